# Optimizing a Trainium2 kernel written in Bass

```python
import math
import jax, jax.numpy as jnp
from jax import lax
import numpy as np

D_MODEL = 1024
BATCH = 16
SEQ = 4096
DEPTH = 4

N_MIXERS = 3
N_A_LAYERS = (DEPTH + 2) // 3
N_B_LAYERS = (DEPTH + 1) // 3
N_C_LAYERS = DEPTH // 3
N_MOD = 9
D_FF = 2816
HEAD_DIM = 64
BLOCK = 128
NORM_EPS = 1e-6
DA_HEADS = D_MODEL // (2 * HEAD_DIM)
DA_V_DIM = 2 * HEAD_DIM
SUBLN_EPS = 1e-5
RW_HEADS = D_MODEL // HEAD_DIM
RW_DECAY_LORA = 64
RW_ICLR_LORA = 64
RW_GATE_LORA = 160
RW_LNX_EPS = 64e-5
SW_Q_HEADS = D_MODEL // HEAD_DIM
SW_KV_HEADS = 2
SW_GROUP = SW_Q_HEADS // SW_KV_HEADS
SW_WINDOW = 128
SW_Q_DIM = SW_Q_HEADS * HEAD_DIM
SW_KV_DIM = SW_KV_HEADS * HEAD_DIM

kernel_name = "hybrid_diffattn_rwkv7_swasink_macaron"


def rms_norm(x, g, eps=NORM_EPS):
    xf = x.astype(jnp.float32)
    y = xf * lax.rsqrt(jnp.mean(xf * xf, axis=-1, keepdims=True) + eps)
    return (y * g.astype(jnp.float32)).astype(x.dtype)


def modulate(h, shift, scale):
    return h * (1 + scale) + shift


def swiglu(h, w_in, w_out):
    gate, up = jnp.split(h @ w_in, 2, axis=-1)
    return (jax.nn.silu(gate) * up) @ w_out


def diff_lambda_init(layer):
    return 0.8 - 0.6 * math.exp(-0.3 * layer)


def differential_attention(h, w_qkv, w_o, q_norm_g, k_norm_g, lambdas, subln_g, lambda_init):
    B, T, _ = h.shape
    nb = T // BLOCK
    q, k, v = jnp.split(h @ w_qkv, 3, axis=-1)
    q = rms_norm(q.reshape(B, T, DA_HEADS, 2, HEAD_DIM), q_norm_g)
    k = rms_norm(k.reshape(B, T, DA_HEADS, 2, HEAD_DIM), k_norm_g)
    v = v.reshape(B, T, DA_HEADS, DA_V_DIM)
    lam = lambdas.astype(jnp.float32)
    lam_full = jnp.exp(jnp.sum(lam[0] * lam[1])) - jnp.exp(jnp.sum(lam[2] * lam[3])) + lambda_init
    scale = HEAD_DIM ** -0.5
    key_pos = jnp.arange(T)
    q_blocks = jnp.moveaxis(q.reshape(B, nb, BLOCK, DA_HEADS, 2, HEAD_DIM), 1, 0)

    def one_block(args):
        q_blk, n = args
        s = jnp.einsum('bqhcd,bkhcd->bhcqk', q_blk, k).astype(jnp.float32) * scale
        q_pos = n * BLOCK + jnp.arange(BLOCK)
        causal = key_pos[None, :] <= q_pos[:, None]
        p = jax.nn.softmax(jnp.where(causal, s, -jnp.inf), axis=-1)
        attn = (p[:, :, 0] - lam_full * p[:, :, 1]).astype(v.dtype)
        return jnp.einsum('bhqk,bkhe->bqhe', attn, v)

    o = lax.map(one_block, (q_blocks, jnp.arange(nb)))
    o = jnp.moveaxis(o, 0, 1).reshape(B, T, DA_HEADS, DA_V_DIM)
    o = rms_norm(o, subln_g, SUBLN_EPS) * (1.0 - lambda_init)
    return o.reshape(B, T, DA_HEADS * DA_V_DIM) @ w_o


def rwkv7_scan(r, w, k, v, a, b):
    B, T, H, N = r.shape

    def step(S, inp):
        r_t, w_t, k_t, v_t, a_t, b_t = inp
        sa = jnp.einsum('bhvk,bhk->bhv', S, a_t)
        S = S * w_t[:, :, None, :] + sa[..., None] * b_t[:, :, None, :] + v_t[..., None] * k_t[:, :, None, :]
        return S, jnp.einsum('bhvk,bhk->bhv', S, r_t)

    S0 = jnp.zeros((B, H, N, N), jnp.float32)
    xs = (jnp.moveaxis(r, 1, 0), jnp.moveaxis(w, 1, 0), jnp.moveaxis(k, 1, 0),
          jnp.moveaxis(v, 1, 0), jnp.moveaxis(a, 1, 0), jnp.moveaxis(b, 1, 0))
    _, o = lax.scan(step, S0, xs)
    return jnp.moveaxis(o, 0, 1)


def rwkv7_time_mix(h, mu, w_rkv, w_o, decay_w0, decay_w1, decay_w2, iclr_a0, iclr_a1, iclr_a2,
                   gate_g1, gate_g2, k_k, k_a, r_k, lnx_g, lnx_b):
    B, T, D = h.shape
    f32 = jnp.float32
    xx = jnp.pad(h[:, :-1], ((0, 0), (1, 0), (0, 0))) - h
    mixed = h[None] + xx[None] * mu[:, None, None, :]
    r, k, v = jnp.einsum('nbtd,nde->nbte', mixed[:3], w_rkv)
    xw, xa, xg = mixed[3], mixed[4], mixed[5]
    w_log = -jax.nn.softplus(-(decay_w0 + jnp.tanh(xw @ decay_w1) @ decay_w2)) - 0.5
    decay = jnp.exp(-jnp.exp(w_log.astype(f32)))
    a = jax.nn.sigmoid(iclr_a0 + (xa @ iclr_a1) @ iclr_a2)
    g = jax.nn.sigmoid(xg @ gate_g1) @ gate_g2
    hd = lambda t: t.reshape(B, T, RW_HEADS, HEAD_DIM).astype(f32)
    kk = hd(k * k_k)
    kk = kk / jnp.maximum(jnp.sqrt(jnp.sum(kk * kk, axis=-1, keepdims=True)), 1e-12)
    k = k * (1 + (a - 1) * k_a)
    r_h, k_h, v_h, a_h = hd(r), hd(k), hd(v), hd(a)
    o = rwkv7_scan(r_h, decay.reshape(B, T, RW_HEADS, HEAD_DIM), k_h, v_h, -kk, kk * a_h)
    mean = jnp.mean(o, axis=-1, keepdims=True)
    var = jnp.mean(jnp.square(o - mean), axis=-1, keepdims=True)
    o = ((o - mean) * lax.rsqrt(var + RW_LNX_EPS)).reshape(B, T, D) * lnx_g.astype(f32) + lnx_b.astype(f32)
    bonus = jnp.sum(r_h * k_h * r_k.astype(f32), axis=-1, keepdims=True) * v_h
    o = (o + bonus.reshape(B, T, D)).astype(h.dtype)
    return (o * g) @ w_o


def sliding_window_sink_attention(h, w_qkv, w_o, q_norm_g, k_norm_g, sinks):
    B, T, _ = h.shape
    nb = T // BLOCK
    q, k, v = jnp.split(h @ w_qkv, [SW_Q_DIM, SW_Q_DIM + SW_KV_DIM], axis=-1)
    q = rms_norm(q.reshape(B, nb, BLOCK, SW_KV_HEADS, SW_GROUP, HEAD_DIM), q_norm_g)
    k = rms_norm(k.reshape(B, nb, BLOCK, SW_KV_HEADS, HEAD_DIM), k_norm_g)
    v = v.reshape(B, nb, BLOCK, SW_KV_HEADS, HEAD_DIM)

    def band(t):
        prev = jnp.pad(t[:, :-1], ((0, 0), (1, 0), (0, 0), (0, 0), (0, 0)))
        return jnp.concatenate([prev, t], axis=2)

    kb, vb = band(k), band(v)
    s = jnp.einsum('bnqhgd,bnkhd->bnhgqk', q, kb).astype(jnp.float32) * (HEAD_DIM ** -0.5)
    qi = jnp.arange(BLOCK)[:, None]
    ki = jnp.arange(2 * BLOCK)[None, :]
    rel = qi + BLOCK - ki
    blk = jnp.arange(nb)[:, None, None]
    valid = (rel >= 0) & (rel < SW_WINDOW) & (blk * BLOCK + ki - BLOCK >= 0)
    s = jnp.where(valid[None, :, None, None], s, -jnp.inf)
    sink = sinks.astype(jnp.float32).reshape(SW_KV_HEADS, SW_GROUP)[None, None, :, :, None, None]
    m = jnp.maximum(jnp.max(s, axis=-1, keepdims=True), sink)
    p = jnp.exp(s - m)
    p = (p / (jnp.sum(p, axis=-1, keepdims=True) + jnp.exp(sink - m))).astype(v.dtype)
    o = jnp.einsum('bnhgqk,bnkhd->bnqhgd', p, vb).reshape(B, T, SW_Q_DIM)
    return o @ w_o


def _normal(key, shape, scale):
    return jax.random.normal(key, shape, jnp.float32) * scale


def _gain(key, shape):
    return 1.0 + 0.02 * jax.random.normal(key, shape, jnp.float32)


def setup_inputs(seed: int = 0) -> dict:
    key = jax.random.key(seed)
    ks = iter(jax.random.split(key, 40))
    D, L = D_MODEL, DEPTH
    NA, NB, NC = N_A_LAYERS, N_B_LAYERS, N_C_LAYERS
    return {
        "x": _normal(next(ks), (BATCH, SEQ, D), 1.0),
        "c": _normal(next(ks), (BATCH, D), 1.0),
        "norm_g": _gain(next(ks), (L, 3, D)),
        "ada_w": _normal(next(ks), (L, D, N_MOD * D), 0.5 * D ** -0.5),
        "ada_b": _normal(next(ks), (L, N_MOD * D), 0.02),
        "ffn_w_in": _normal(next(ks), (L, 2, D, 2 * D_FF), D ** -0.5),
        "ffn_w_out": _normal(next(ks), (L, 2, D_FF, D), D_FF ** -0.5),
        "da_w_qkv": _normal(next(ks), (NA, D, 3 * D), D ** -0.5),
        "da_w_o": _normal(next(ks), (NA, D, D), D ** -0.5),
        "da_q_norm_g": _gain(next(ks), (NA, HEAD_DIM)),
        "da_k_norm_g": _gain(next(ks), (NA, HEAD_DIM)),
        "da_lambda": _normal(next(ks), (NA, 4, HEAD_DIM), 0.1),
        "da_subln_g": _gain(next(ks), (NA, DA_V_DIM)),
        "rw_mu": jax.random.uniform(next(ks), (NB, 6, D), jnp.float32),
        "rw_w_rkv": _normal(next(ks), (NB, 3, D, D), D ** -0.5),
        "rw_w_o": _normal(next(ks), (NB, D, D), D ** -0.5),
        "rw_decay_w0": jax.random.uniform(next(ks), (NB, D), jnp.float32, -6.5, -1.5),
        "rw_decay_w1": _normal(next(ks), (NB, D, RW_DECAY_LORA), D ** -0.5),
        "rw_decay_w2": _normal(next(ks), (NB, RW_DECAY_LORA, D), 0.5 * RW_DECAY_LORA ** -0.5),
        "rw_iclr_a0": _normal(next(ks), (NB, D), 0.1),
        "rw_iclr_a1": _normal(next(ks), (NB, D, RW_ICLR_LORA), D ** -0.5),
        "rw_iclr_a2": _normal(next(ks), (NB, RW_ICLR_LORA, D), 0.5 * RW_ICLR_LORA ** -0.5),
        "rw_gate_g1": _normal(next(ks), (NB, D, RW_GATE_LORA), D ** -0.5),
        "rw_gate_g2": _normal(next(ks), (NB, RW_GATE_LORA, D), RW_GATE_LORA ** -0.5),
        "rw_k_k": 0.85 + 0.02 * jax.random.normal(next(ks), (NB, D), jnp.float32),
        "rw_k_a": _gain(next(ks), (NB, D)),
        "rw_r_k": _normal(next(ks), (NB, RW_HEADS, HEAD_DIM), 0.1),
        "rw_lnx_g": _gain(next(ks), (NB, D)),
        "rw_lnx_b": _normal(next(ks), (NB, D), 0.02),
        "sw_w_qkv": _normal(next(ks), (NC, D, SW_Q_DIM + 2 * SW_KV_DIM), D ** -0.5),
        "sw_w_o": _normal(next(ks), (NC, SW_Q_DIM, D), SW_Q_DIM ** -0.5),
        "sw_q_norm_g": _gain(next(ks), (NC, HEAD_DIM)),
        "sw_k_norm_g": _gain(next(ks), (NC, HEAD_DIM)),
        "sw_sinks": _normal(next(ks), (NC, SW_Q_HEADS), 1.0),
    }


def reference(x, c, norm_g, ada_w, ada_b, ffn_w_in, ffn_w_out,
              da_w_qkv, da_w_o, da_q_norm_g, da_k_norm_g, da_lambda, da_subln_g,
              rw_mu, rw_w_rkv, rw_w_o, rw_decay_w0, rw_decay_w1, rw_decay_w2,
              rw_iclr_a0, rw_iclr_a1, rw_iclr_a2, rw_gate_g1, rw_gate_g2,
              rw_k_k, rw_k_a, rw_r_k, rw_lnx_g, rw_lnx_b,
              sw_w_qkv, sw_w_o, sw_q_norm_g, sw_k_norm_g, sw_sinks):
    cond = jax.nn.silu(c)
    for i in range(DEPTH):
        mod = (cond @ ada_w[i] + ada_b[i])[:, None, :]
        sh1, sc1, g1, sh2, sc2, g2, sh3, sc3, g3 = jnp.split(mod, N_MOD, axis=-1)
        h = modulate(rms_norm(x, norm_g[i, 0]), sh1, sc1)
        x = x + 0.5 * g1 * swiglu(h, ffn_w_in[i, 0], ffn_w_out[i, 0])
        h = modulate(rms_norm(x, norm_g[i, 1]), sh2, sc2)
        kind, j = i % N_MIXERS, i // N_MIXERS
        if kind == 0:
            y = differential_attention(h, da_w_qkv[j], da_w_o[j], da_q_norm_g[j], da_k_norm_g[j],
                                       da_lambda[j], da_subln_g[j], diff_lambda_init(i))
        elif kind == 1:
            y = rwkv7_time_mix(h, rw_mu[j], rw_w_rkv[j], rw_w_o[j], rw_decay_w0[j], rw_decay_w1[j],
                               rw_decay_w2[j], rw_iclr_a0[j], rw_iclr_a1[j], rw_iclr_a2[j],
                               rw_gate_g1[j], rw_gate_g2[j], rw_k_k[j], rw_k_a[j], rw_r_k[j],
                               rw_lnx_g[j], rw_lnx_b[j])
        else:
            y = sliding_window_sink_attention(h, sw_w_qkv[j], sw_w_o[j], sw_q_norm_g[j],
                                              sw_k_norm_g[j], sw_sinks[j])
        x = x + g2 * y
        h = modulate(rms_norm(x, norm_g[i, 2]), sh3, sc3)
        x = x + 0.5 * g3 * swiglu(h, ffn_w_in[i, 1], ffn_w_out[i, 1])
    return x
```

```python
import math
from contextlib import ExitStack

import numpy as np
import concourse.bass as bass
import concourse.mybir as mybir
from concourse.bass_utils import run_bass_kernel_spmd

F32 = mybir.dt.float32
BF16 = mybir.dt.bfloat16
I32 = mybir.dt.int32
AF = mybir.ActivationFunctionType
ALU = mybir.AluOpType
AX = mybir.AxisListType

NCORES = 8
NS = 2
T = 4096
NT = NS * T
D = 1024
DC = 8
DFF = 2816
FC = 22
DEPTH = 4
EPS = 1e-6
DBG_EXT = False
RW_STOP = 0
USE_ARS = False


class Buf:
    __slots__ = ("w", "r", "rec", "name")

    def __init__(self, name=""):
        self.w = None
        self.r = {}
        self.rec = None
        self.name = name


class K:
    def __init__(self, nc, es, same_engine_sync=False):
        self.nc = nc
        self.es = es
        self.pes = es
        self.eng = {"pe": nc.tensor, "act": nc.scalar, "dve": nc.vector, "pool": nc.gpsimd, "sp": nc.sync}
        self.esem = {e: es.enter_context(nc.semaphore("s_" + e)) for e in ("pe", "act", "dve", "pool")}
        self.ecnt = {e: 0 for e in self.esem}
        self.waited = {e: {} for e in self.eng}
        self.same = same_engine_sync
        self.recs = []
        self.free_recs = []
        self.phase_recs = []
        self.nalloc = 0

    def sb(self, shape, dt, name=None):
        self.nalloc += 1
        return self.pes.enter_context(self.nc.sbuf_tensor(name or ("t%d" % self.nalloc), list(shape), dt))

    def psum(self, shape, dt, name=None):
        self.nalloc += 1
        return self.pes.enter_context(self.nc.psum_tensor(name or ("p%d" % self.nalloc), list(shape), dt))

    def _wait(self, e, tok):
        if tok is None:
            return
        key, sem, val, owner = tok
        if owner == e and (e == "pe" or not self.same):
            return
        d = self.waited[e]
        if d.get(key, 0) >= val:
            return
        self.eng[e].wait_ge(sem, val)
        d[key] = val

    def _deps(self, e, reads, writes):
        for b in reads:
            self._wait(e, b.w)
        for b in writes:
            self._wait(e, b.w)
            for t in b.r.values():
                self._wait(e, t)

    def _mark(self, tok, reads, writes):
        for b in reads:
            b.r[tok[0]] = tok
        for b in writes:
            b.w = tok
            b.r = {}

    def op(self, e, ins_fn, reads=(), writes=()):
        self._deps(e, reads, writes)
        ins = ins_fn(self.eng[e])
        self.ecnt[e] += 1
        ins.then_inc(self.esem[e], 1)
        tok = (e, self.esem[e], self.ecnt[e], e)
        self._mark(tok, reads, writes)
        return tok

    def dma(self, q, out, in_, owner, reads=(), writes=(), **kw):
        self._deps(q, reads, writes)
        if owner.rec is None:
            if self.free_recs:
                owner.rec = self.free_recs.pop()
            else:
                owner.rec = [self.es.enter_context(self.nc.semaphore("d%d" % len(self.recs))), 0, len(self.recs)]
                self.recs.append(owner.rec)
            self.phase_recs.append(owner.rec)
        rec = owner.rec
        self.eng[q].dma_start(out=out, in_=in_, **kw).then_inc(rec[0], 16)
        rec[1] += 16
        tok = ("d%d" % rec[2], rec[0], rec[1], "dma")
        self._mark(tok, reads, writes)
        return tok

    def barrier(self):
        for e in self.eng:
            for o in self.esem:
                if o != e and self.ecnt[o] > 0:
                    self._wait(e, (o, self.esem[o], self.ecnt[o], o))
            for rec in self.recs:
                if rec[1] > 0:
                    self._wait(e, ("d%d" % rec[2], rec[0], rec[1], "dma"))
        self.free_recs.extend(self.phase_recs)
        self.phase_recs = []

    def mm(self, out, lhsT, rhs, start, stop, reads, writes, **kw):
        return self.op("pe", lambda e: e.matmul(out, lhsT, rhs, start=start, stop=stop, **kw), reads, writes)

    def tr(self, out, in_, ident, reads, writes):
        return self.op("pe", lambda e: e.transpose(out, in_, ident), reads, writes)

    def act(self, out, in_, func, reads, writes, bias=None, scale=None, eng="act"):
        kw = {}
        if bias is not None:
            kw["bias"] = bias
        if scale is not None:
            kw["scale"] = scale
        return self.op(eng, lambda e: e.activation(out, in_, func, **kw), reads, writes)

    def ts(self, eng, out, in0, s1, s2, op0, op1, reads, writes):
        if op1 is None:
            return self.op(eng, lambda e: e.tensor_scalar(out, in0, s1, None, op0=op0), reads, writes)
        return self.op(eng, lambda e: e.tensor_scalar(out, in0, s1, s2, op0=op0, op1=op1), reads, writes)

    def tt(self, eng, out, in0, in1, op, reads, writes):
        return self.op(eng, lambda e: e.tensor_tensor(out, in0, in1, op), reads, writes)

    def stt(self, out, in0, scalar, in1, op0, op1, reads, writes):
        return self.op("dve", lambda e: e.scalar_tensor_tensor(out, in0, scalar, in1, op0=op0, op1=op1), reads, writes)

    def copy(self, eng, out, in_, reads, writes):
        if eng == "act":
            return self.op("act", lambda e: e.copy(out, in_), reads, writes)
        return self.op(eng, lambda e: e.tensor_copy(out, in_), reads, writes)

    def recip(self, out, in_, reads, writes):
        return self.op("dve", lambda e: e.reciprocal(out, in_), reads, writes)

    def memset(self, eng, out, val, writes):
        return self.op(eng, lambda e: e.memset(out, val), (), writes)


class Ctx:
    pass


def build(stop_after=None, same_engine_sync=True, dbg_mod=False, phase_list=None):
    nc = bass.Bass("TRN2", target_bir_lowering=False)
    g = Ctx()
    g.nc = nc

    def din(name, shape):
        return nc.dram_tensor(name, list(shape), F32, kind="ExternalInput").ap()

    g.x = din("x", [NS, T, D])
    g.c = din("c", [NS, D])
    g.norm_g = din("norm_g", [DEPTH, 3, D])
    g.ada_w = din("ada_w", [DEPTH, D, 9 * D])
    g.ada_b = din("ada_b", [DEPTH, 9 * D])
    g.ffn_w_in = din("ffn_w_in", [DEPTH, 2, D, 2 * DFF])
    g.ffn_w_out = din("ffn_w_out", [DEPTH, 2, DFF, D])
    g.da_w_qkv = din("da_w_qkv", [2, D, 3 * D])
    g.da_w_o = din("da_w_o", [2, D, D])
    g.da_q_norm_g = din("da_q_norm_g", [2, 64])
    g.da_k_norm_g = din("da_k_norm_g", [2, 64])
    g.da_lambda = din("da_lambda", [2, 4, 64])
    g.da_subln_g = din("da_subln_g", [2, 128])
    g.rw_mu = din("rw_mu", [1, 6, D])
    g.rw_w_rkv = din("rw_w_rkv", [1, 3, D, D])
    g.rw_w_o = din("rw_w_o", [1, D, D])
    g.rw_decay_w0 = din("rw_decay_w0", [1, D])
    g.rw_decay_w1 = din("rw_decay_w1", [1, D, 64])
    g.rw_decay_w2 = din("rw_decay_w2", [1, 64, D])
    g.rw_iclr_a0 = din("rw_iclr_a0", [1, D])
    g.rw_iclr_a1 = din("rw_iclr_a1", [1, D, 64])
    g.rw_iclr_a2 = din("rw_iclr_a2", [1, 64, D])
    g.rw_gate_g1 = din("rw_gate_g1", [1, D, 160])
    g.rw_gate_g2 = din("rw_gate_g2", [1, 160, D])
    g.rw_k_k = din("rw_k_k", [1, D])
    g.rw_k_a = din("rw_k_a", [1, D])
    g.rw_r_k = din("rw_r_k", [1, 16, 64])
    g.rw_lnx_g = din("rw_lnx_g", [1, D])
    g.rw_lnx_b = din("rw_lnx_b", [1, D])
    g.sw_w_qkv = din("sw_w_qkv", [1, D, 1280])
    g.sw_w_o = din("sw_w_o", [1, D, D])
    g.sw_q_norm_g = din("sw_q_norm_g", [1, 64])
    g.sw_k_norm_g = din("sw_k_norm_g", [1, 64])
    g.sw_sinks = din("sw_sinks", [1, 16])
    g.y = nc.dram_tensor("y", [NS, T, D], F32, kind="ExternalOutput").ap()
    g.xT = nc.dram_tensor("xT_scratch", [D, NT], F32, kind="Internal").ap()
    g.xTv = g.xT.rearrange("(c p) t -> p c t", p=128)
    if dbg_mod:
        g.dbg = nc.dram_tensor("dbg", [128, DEPTH * NS * 72], F32, kind="ExternalOutput").ap()

    with ExitStack() as es:
        k = K(nc, es, same_engine_sync)
        g.k = k
        g.bxT = [Buf("xT%d" % i) for i in range(NT // 256)]
        phase_consts(g)
        phases = [("tin", None)]
        for i in range(DEPTH):
            phases.append(("ffn", (i, 0)))
            phases.append(("mix", i))
            phases.append(("ffn", (i, 1)))
        phases.append(("tout", None))
        if phase_list is not None:
            phases = phase_list
        n = 0
        for kind, arg in phases:
            if stop_after is not None and n > stop_after and kind != "tout":
                continue
            n += 1
            with ExitStack() as pes:
                k.pes = pes
                if kind == "tin":
                    phase_tin(g)
                elif kind == "tout":
                    phase_tout(g)
                elif kind == "ffn":
                    phase_ffn(g, *arg)
                elif kind == "mix":
                    phase_mix(g, arg)
                k.barrier()
            k.pes = es
        if dbg_mod:
            b = Buf()
            k.dma("sp", g.dbg, g.modT[:].rearrange("p a b c -> p (a b c)"), b, reads=[g.bmod], writes=[b])
            k.barrier()
    return nc


def phase_consts(g):
    k = g.k
    nc = g.nc
    g.ident = k.sb([128, 128], F32, "ident")
    g.identb = k.sb([128, 128], BF16, "identb")
    g.onesb = k.sb([128, 128], BF16, "onesb")
    g.bconst = Buf("const")
    g.modT = k.sb([128, DEPTH, NS, 72], F32, "modT")
    g.gm = k.sb([128, DEPTH, NS, 3, 8], F32, "gm")
    g.hg = k.sb([128, DEPTH, NS, 3, 8], F32, "hg")
    g.bmod = Buf("mod")
    g.blk64 = k.sb([128, 128], BF16, "blk64")
    k.memset("dve", g.blk64[:], 0.0, [g.bconst])
    k.memset("dve", g.blk64[0:64, 0:64], 1.0, [g.bconst])
    k.memset("dve", g.blk64[64:128, 64:128], 1.0, [g.bconst])
    g.epsc = k.sb([128, 4], F32, "epsc")
    for q, v in enumerate((EPS, 1e-5, 64e-5, 1e-24)):
        k.memset("dve", g.epsc[:, q:q + 1], v, [g.bconst])
    with ExitStack() as pes:
        k.pes = pes
        io = k.sb([128, 128], I32, "iota")
        bio = Buf()
        k.op("pool", lambda e: e.iota(io[:], [[1, 128]], base=0, channel_multiplier=-1), (), [bio])
        k.ts("dve", g.ident[:], io[:], 0, None, ALU.is_equal, None, [bio], [g.bconst])
        k.copy("dve", g.identb[:], g.ident[:], [g.bconst], [g.bconst])
        k.memset("dve", g.onesb[:], 1.0, [g.bconst])
        csb = k.sb([NS, D], F32)
        bc = Buf()
        k.dma("sp", csb[:], g.c, bc, writes=[bc])
        csl = k.sb([NS, D], F32)
        bcs = Buf()
        k.act(csl[:], csb[:], AF.Silu, [bc], [bcs])
        pst = k.psum([128, 512], F32)
        bps = Buf()
        for c in range(8):
            k.tr(pst[:, c * NS:(c + 1) * NS], csl[:, c * 128:(c + 1) * 128], g.ident[:NS, :NS], [bcs, g.bconst], [bps])
        condT = k.sb([128, 8, NS], F32)
        bcond = Buf()
        k.copy("dve", condT[:].rearrange("p c s -> p (c s)"), pst[:, :8 * NS], [bps], [bcond])
        adabT = k.sb([128, DEPTH * 72], F32)
        ngT = k.sb([128, DEPTH * 3 * 8], F32)
        bab = Buf()
        rows = k.sb([96, 4, 128], F32)
        brow = Buf()
        abv = g.ada_b.rearrange("l (j p) -> (l j) p", p=128)
        for q in range(3):
            k.dma("sp", rows[:, q, :], abv[q * 96:(q + 1) * 96, :], brow, writes=[brow])
        k.dma("sp", rows[:, 3, :], g.norm_g.rearrange("l k (c p) -> (l k c) p", p=128), brow, writes=[brow])
        pst2 = k.psum([128, 512], F32)
        bps2 = Buf()
        for q in range(4):
            k.tr(pst2[:, q * 96:(q + 1) * 96], rows[:, q, :], g.ident[:96, :96], [brow, g.bconst], [bps2])
        k.copy("dve", adabT[:], pst2[:, :288], [bps2], [bab])
        k.copy("dve", ngT[:], pst2[:, 288:384], [bps2], [bab])
        NPC = 8
        PW = 9 * D // NPC
        wbuf = [k.sb([128, 8, PW], F32) for _ in range(2)]
        bwb = [Buf(), Buf()]
        psm = [k.psum([128, 512], F32) for _ in range(2)]
        bpsm = [Buf(), Buf()]
        it = 0
        for i in range(DEPTH):
            wv = g.ada_w[i].rearrange("(c p) n -> p c n", p=128)
            for q in range(NPC):
                s = it % 2
                for half in range(2):
                    k.dma("sp", wbuf[s][:, half * 4:(half + 1) * 4, :], wv[:, half * 4:(half + 1) * 4, q * PW:(q + 1) * PW],
                          bwb[s], writes=[bwb[s]])
                for jj in range(9):
                    for c in range(8):
                        k.mm(psm[s][:, jj * NS:(jj + 1) * NS], wbuf[s][:, c, jj * 128:(jj + 1) * 128], condT[:, c, :],
                             c == 0, c == 7, [bwb[s], bcond], [bpsm[s]])
                for sq in range(NS):
                    k.tt("dve", g.modT[:, i, sq, q * 9:(q + 1) * 9], psm[s][:, sq:9 * NS:NS],
                         adabT[:, i * 72 + q * 9:i * 72 + (q + 1) * 9], ALU.add, [bpsm[s], bab], [g.bmod])
                it += 1
        for i in range(DEPTH):
            for sq in range(NS):
                for n in range(3):
                    k.stt(g.gm[:, i, sq, n, :], g.modT[:, i, sq, (3 * n + 1) * 8:(3 * n + 2) * 8], 1.0,
                          ngT[:, (i * 3 + n) * 8:(i * 3 + n + 1) * 8], ALU.add, ALU.mult, [g.bmod, bab], [g.bmod])
                    k.ts("dve", g.hg[:, i, sq, n, :], g.modT[:, i, sq, (3 * n + 2) * 8:(3 * n + 3) * 8],
                         0.5 if n != 1 else 1.0, None, ALU.mult, None, [g.bmod], [g.bmod])
        k.barrier()
    k.pes = k.es


def shiftv(g, i, sq, n):
    return g.modT[:, i, sq, (3 * n) * 8:(3 * n + 1) * 8]


def phase_tin(g):
    k = g.k
    xtok = [k.sb([128, 2, D], F32) for _ in range(2)]
    bxtok = [Buf(), Buf()]
    xt = [k.sb([128, 8, 256], F32) for _ in range(2)]
    bxt = [Buf(), Buf()]
    ps = [k.psum([128, 512], F32) for _ in range(8)]
    bps = [Buf() for _ in range(8)]
    for b in range(NT // 256):
        s = b % 2
        sq, t0 = divmod(b * 256, T)
        k.dma("sp", xtok[s][:], g.x[sq, t0:t0 + 256, :].rearrange("(j p) d -> p j d", p=128), bxtok[s], writes=[bxtok[s]])
        for c in range(8):
            bank = s * 4 + c // 2
            for j in range(2):
                k.tr(ps[bank][:, (c % 2) * 256 + j * 128:(c % 2) * 256 + (j + 1) * 128], xtok[s][:, j, c * 128:(c + 1) * 128],
                     g.ident[:], [bxtok[s], g.bconst], [bps[bank]])
        for q in range(4):
            bank = s * 4 + q
            k.copy("act" if q % 2 else "dve", xt[s][:, 2 * q:2 * q + 2, :].rearrange("p c t -> p (c t)"), ps[bank][:],
                   [bps[bank]], [bxt[s]])
        k.dma("sp", g.xTv[:, :, b * 256:(b + 1) * 256], xt[s][:], bxt[s], reads=[bxt[s]], writes=[g.bxT[b]])


def phase_tout(g):
    k = g.k
    xt = [k.sb([128, 8, 256], F32) for _ in range(2)]
    bxt = [Buf(), Buf()]
    ytok = [k.sb([128, 2, D], F32) for _ in range(2)]
    bytok = [Buf(), Buf()]
    ps = [k.psum([128, 512], F32) for _ in range(8)]
    bps = [Buf() for _ in range(8)]
    by = Buf("y")
    for b in range(NT // 256):
        s = b % 2
        sq, t0 = divmod(b * 256, T)
        k.dma("sp", xt[s][:], g.xTv[:, :, b * 256:(b + 1) * 256], bxt[s], reads=[g.bxT[b]], writes=[bxt[s]])
        for j in range(2):
            for c in range(8):
                bank = s * 4 + j * 2 + c // 4
                k.tr(ps[bank][:, (c % 4) * 128:(c % 4 + 1) * 128], xt[s][:, c, j * 128:(j + 1) * 128], g.ident[:],
                     [bxt[s], g.bconst], [bps[bank]])
        for q in range(4):
            bank = s * 4 + q
            k.copy("act" if q % 2 else "dve", ytok[s][:, q // 2, (q % 2) * 512:(q % 2 + 1) * 512], ps[bank][:], [bps[bank]], [bytok[s]])
        k.dma("sp", g.y[sq, t0:t0 + 256, :].rearrange("(j p) d -> p j d", p=128), ytok[s][:], bytok[s], reads=[bytok[s]], writes=[by])


def norm_mod(g, xt, bxt, hT, bhT, i, sq, n, N, st):
    k = g.k
    for c in range(8):
        s2 = c % 2
        k.act(st["sq"][s2][:, :N], xt[:, c, :N], AF.Square, [bxt], [st["bsq"][s2]])
        k.mm(st["ps"][:, :N], g.onesb[:], st["sq"][s2][:, :N], c == 0, c == 7, [st["bsq"][s2], g.bconst], [st["bps"]])
    if USE_ARS:
        k.act(st["rstd"][:, :N], st["ps"][:, :N], AF.Abs_reciprocal_sqrt, [st["bps"], g.bconst], [st["brstd"]], bias=g.epsc[:, 0:1], scale=1.0 / D)
    else:
        k.act(st["rt"][:, :N], st["ps"][:, :N], AF.Sqrt, [st["bps"], g.bconst], [st["brt"]], bias=g.epsc[:, 0:1], scale=1.0 / D)
        k.recip(st["rstd"][:, :N], st["rt"][:, :N], [st["brt"]], [st["brstd"]])
    for c in range(8):
        s2 = c % 2
        k.stt(st["tmp"][s2][:, :N], xt[:, c, :N], g.gm[:, i, sq, n, c:c + 1], st["rstd"][:, :N], ALU.mult, ALU.mult,
              [bxt, st["brstd"], g.bmod], [st["btmp"][s2]])
        k.act(hT[:, c, :N], st["tmp"][s2][:, :N], AF.Identity, [st["btmp"][s2], g.bmod], [bhT],
              bias=g.modT[:, i, sq, 3 * n * 8 + c:3 * n * 8 + c + 1])


def norm_scratch(g, N):
    k = g.k
    st = {}
    st["sq"] = [k.sb([128, N], BF16) for _ in range(2)]
    st["bsq"] = [Buf(), Buf()]
    st["ps"] = k.psum([128, 512], F32)
    st["bps"] = Buf()
    st["rt"] = k.sb([128, N], F32)
    st["brt"] = Buf()
    st["rstd"] = k.sb([128, N], F32)
    st["brstd"] = Buf()
    st["tmp"] = [k.sb([128, N], F32) for _ in range(2)]
    st["btmp"] = [Buf(), Buf()]
    return st


def phase_ffn(g, i, which):
    k = g.k
    n = 0 if which == 0 else 2
    N = 256
    w_in = k.sb([128, 8, 2 * DFF], BF16)
    w_out = k.sb([128, FC, D], BF16)
    bwi, bwo = Buf("w_in"), Buf("w_out")
    wiv = g.ffn_w_in[i, which].rearrange("(c p) f -> p c f", p=128)
    wov = g.ffn_w_out[i, which].rearrange("(j p) d -> p j d", p=128)
    for c in range(8):
        k.dma("pool", w_in[:, c, :], wiv[:, c, :], bwi, writes=[bwi])
    for q in range(2):
        k.dma("pool", w_out[:, q * 11:(q + 1) * 11, :], wov[:, q * 11:(q + 1) * 11, :], bwo, writes=[bwo])
    xt = [k.sb([128, 8, N], F32) for _ in range(2)]
    bxt = [Buf(), Buf()]
    hT = [k.sb([128, 8, N], BF16) for _ in range(2)]
    bhT = [Buf(), Buf()]
    actT = [k.sb([128, FC, N], BF16) for _ in range(2)]
    bact = [Buf(), Buf()]
    sg = [k.sb([128, N], F32) for _ in range(2)]
    bsg = [Buf(), Buf()]
    st = norm_scratch(g, N)
    psg = [k.psum([128, 512], F32) for _ in range(2)]
    psu = [k.psum([128, 512], F32) for _ in range(2)]
    pso = [k.psum([128, 512], F32) for _ in range(2)]
    bpsg, bpsu, bpso = [Buf(), Buf()], [Buf(), Buf()], [Buf(), Buf()]
    nb = NT // N

    def load(b):
        s = b % 2
        k.dma("sp", xt[s][:], g.xTv[:, :, b * N:(b + 1) * N], bxt[s], reads=[g.bxT[b]], writes=[bxt[s]])

    load(0)
    norm_mod(g, xt[0], bxt[0], hT[0], bhT[0], i, 0, n, N, st)
    for b in range(nb):
        s = b % 2
        sq = (b * N) // T
        if b + 1 < nb:
            load(b + 1)
        for j in range(FC):
            p2 = j % 2
            for c in range(8):
                k.mm(psg[p2][:, :N], w_in[:, c, j * 128:(j + 1) * 128], hT[s][:, c, :], c == 0, c == 7, [bwi, bhT[s]], [bpsg[p2]])
            for c in range(8):
                k.mm(psu[p2][:, :N], w_in[:, c, DFF + j * 128:DFF + (j + 1) * 128], hT[s][:, c, :], c == 0, c == 7,
                     [bwi, bhT[s]], [bpsu[p2]])
            k.act(sg[p2][:], psg[p2][:, :N], AF.Silu, [bpsg[p2]], [bsg[p2]])
            k.tt("dve", actT[s][:, j, :], sg[p2][:], psu[p2][:, :N], ALU.mult, [bsg[p2], bpsu[p2]], [bact[s]])
        for dc in range(8):
            p2 = dc % 2
            if dc == 3 and b + 1 < nb:
                norm_mod(g, xt[1 - s], bxt[1 - s], hT[1 - s], bhT[1 - s], i, ((b + 1) * N) // T, n, N, st)
            for j in range(FC):
                k.mm(pso[p2][:, :N], w_out[:, j, dc * 128:(dc + 1) * 128], actT[s][:, j, :], j == 0, j == FC - 1,
                     [bwo, bact[s]], [bpso[p2]])
            k.stt(xt[s][:, dc, :], pso[p2][:, :N], g.hg[:, i, sq, n, dc:dc + 1], xt[s][:, dc, :], ALU.mult, ALU.add,
                  [bpso[p2], g.bmod, bxt[s]], [bxt[s]])
        k.dma("sp", g.xTv[:, :, b * N:(b + 1) * N], xt[s][:], bxt[s], reads=[bxt[s]], writes=[g.bxT[b]])


def phase_mix(g, i):
    kind = i % 3
    if kind == 0:
        phase_diffattn(g, i)
    elif kind == 1:
        phase_rwkv(g, i)
    else:
        phase_swa(g, i)


def diff_lambda_init(layer):
    return 0.8 - 0.6 * math.exp(-0.3 * layer)


def qk_norm_chunk(g, ps, bps, out, gvec, st, N, reads_extra=()):
    k = g.k
    s2 = st["i"] % 2
    st["i"] += 1
    k.act(st["sq"][s2][:, :N], ps, AF.Square, [bps], [st["bsq"][s2]])
    k.mm(st["pss"][s2][:, :N], g.blk64[:], st["sq"][s2][:, :N], True, True, [st["bsq"][s2], g.bconst], [st["bpss"][s2]])
    k.act(st["rt"][s2][:, :N], st["pss"][s2][:, :N], AF.Sqrt, [st["bpss"][s2], g.bconst], [st["brt"][s2]],
          bias=g.epsc[:, 0:1], scale=1.0 / 64)
    k.recip(st["rs"][s2][:, :N], st["rt"][s2][:, :N], [st["brt"][s2]], [st["brs"][s2]])
    k.stt(out, ps, gvec, st["rs"][s2][:, :N], ALU.mult, ALU.mult, [bps, st["brs"][s2]] + list(reads_extra), st["wout"])


def qk_scratch(g, N, nps=2):
    k = g.k
    st = {"i": 0}
    st["sq"] = [k.sb([128, N], BF16) for _ in range(2)]
    st["bsq"] = [Buf(), Buf()]
    st["pss"] = [k.psum([128, 512], F32) for _ in range(nps)]
    st["bpss"] = [Buf() for _ in range(nps)]
    if nps == 0:
        pass
    elif nps == 1:
        st["pss"] = st["pss"] * 2
        st["bpss"] = st["bpss"] * 2
    st["rt"] = [k.sb([128, N], F32) for _ in range(2)]
    st["brt"] = [Buf(), Buf()]
    st["rs"] = [k.sb([128, N], F32) for _ in range(2)]
    st["brs"] = [Buf(), Buf()]
    return st


def load_vec128(g, dst, src64, bdst):
    k = g.k
    v = src64.rearrange("(p o) -> p o", o=1)
    k.dma("sp", dst[0:64, :], v, bdst, writes=[bdst])
    k.dma("sp", dst[64:128, :], v, bdst, writes=[bdst])


def phase_diffattn(g, i):
    k = g.k
    nc = g.nc
    j = i // 3
    N = 256
    lam_init = diff_lambda_init(i)
    kd = "ExternalOutput" if DBG_EXT else "Internal"
    qT_d = nc.dram_tensor("da_qT%d" % i, [NS, 8, 128, T], BF16, kind=kd).ap()
    kT_d = nc.dram_tensor("da_kT%d" % i, [NS, 8, 128, T], BF16, kind=kd).ap()
    v_d = nc.dram_tensor("da_v%d" % i, [NS, T, D], BF16, kind=kd).ap()
    oT_d = nc.dram_tensor("da_oT%d" % i, [NS, 8, 128, T], BF16, kind=kd).ap()
    nb = NT // N
    bq_d = [Buf() for _ in range(nb)]
    bo_d = [[Buf() for _ in range(8)] for _ in range(NS)]
    outer = k.pes
    with ExitStack() as pes:
        k.pes = pes
        wqk = k.sb([128, 8, 2048], BF16)
        wv = k.sb([128, 8, 1024], BF16)
        bw = Buf()
        wsrc = g.da_w_qkv[j].rearrange("(c p) n -> p c n", p=128)
        for c in range(8):
            k.dma("pool", wqk[:, c, :], wsrc[:, c, 0:2048], bw, writes=[bw])
        for q in range(2):
            k.dma("pool", wv[:, q * 4:(q + 1) * 4, :], wsrc[:, q * 4:(q + 1) * 4, 2048:3072], bw, writes=[bw])
        gq = k.sb([128, 2], F32)
        bgq = Buf()
        load_vec128(g, gq[:, 0:1], g.da_q_norm_g[j], bgq)
        load_vec128(g, gq[:, 1:2], g.da_k_norm_g[j], bgq)
        k.ts("dve", gq[:, 0:1], gq[:, 0:1], 0.125, None, ALU.mult, None, [bgq], [bgq])
        xt = [k.sb([128, 8, N], F32) for _ in range(2)]
        bxt = [Buf(), Buf()]
        hT = [k.sb([128, 8, N], BF16) for _ in range(2)]
        bhT = [Buf(), Buf()]
        qblk = [k.sb([128, 8, N], BF16) for _ in range(2)]
        kblk = [k.sb([128, 8, N], BF16) for _ in range(2)]
        vblk = [k.sb([128, 2, D], BF16) for _ in range(2)]
        bqb, bkb, bvb = [Buf(), Buf()], [Buf(), Buf()], [Buf(), Buf()]
        st = norm_scratch(g, N)
        qs = qk_scratch(g, N)
        psq = [k.psum([128, 512], F32) for _ in range(3)]
        bpsq = [Buf(), Buf(), Buf()]
        psv = [k.psum([128, 512], F32) for _ in range(2)]
        bpsv = [Buf(), Buf()]

        def load(b):
            s = b % 2
            k.dma("sp", xt[s][:], g.xTv[:, :, b * N:(b + 1) * N], bxt[s], reads=[g.bxT[b]], writes=[bxt[s]])

        load(0)
        norm_mod(g, xt[0], bxt[0], hT[0], bhT[0], i, 0, 1, N, st)
        for b in range(nb):
            s = b % 2
            sq, t0 = divmod(b * N, T)
            if b + 1 < nb:
                load(b + 1)
            jobs = []
            for isk in range(2):
                dst, bdst = (qblk[s], bqb[s]) if isk == 0 else (kblk[s], bkb[s])
                for h in range(8):
                    jobs.append((isk * 1024 + h * 128, dst[:, h, :], gq[:, isk:isk + 1], bdst))
            nj = len(jobs)
            for ii in range(nj + 2):
                if ii < nj:
                    col = jobs[ii][0]
                    p3 = ii % 3
                    for c in range(8):
                        k.mm(psq[p3][:, :N], wqk[:, c, col:col + 128], hT[s][:, c, :], c == 0, c == 7, [bw, bhT[s]], [bpsq[p3]])
                if 1 <= ii <= nj:
                    q_ = ii - 1
                    p3, s2 = q_ % 3, q_ % 2
                    k.act(qs["sq"][s2][:, :N], psq[p3][:, :N], AF.Square, [bpsq[p3]], [qs["bsq"][s2]])
                    k.mm(qs["pss"][s2][:, :N], g.blk64[:], qs["sq"][s2][:, :N], True, True, [qs["bsq"][s2], g.bconst], [qs["bpss"][s2]])
                if ii >= 2:
                    q_ = ii - 2
                    p3, s2 = q_ % 3, q_ % 2
                    _, outap, gvec, bdst = jobs[q_]
                    k.act(qs["rt"][s2][:, :N], qs["pss"][s2][:, :N], AF.Sqrt, [qs["bpss"][s2], g.bconst], [qs["brt"][s2]],
                          bias=g.epsc[:, 0:1], scale=1.0 / 64)
                    k.recip(qs["rs"][s2][:, :N], qs["rt"][s2][:, :N], [qs["brt"][s2]], [qs["brs"][s2]])
                    k.stt(outap, psq[p3][:, :N], gvec, qs["rs"][s2][:, :N], ALU.mult, ALU.mult, [bpsq[p3], qs["brs"][s2], bgq], [bdst])
            if b + 1 < nb:
                norm_mod(g, xt[1 - s], bxt[1 - s], hT[1 - s], bhT[1 - s], i, ((b + 1) * N) // T, 1, N, st)
            for jt in range(2):
                for half in range(2):
                    p2 = (jt * 2 + half) % 2
                    for c in range(8):
                        k.mm(psv[p2][:], hT[s][:, c, jt * 128:(jt + 1) * 128], wv[:, c, half * 512:(half + 1) * 512], c == 0, c == 7,
                             [bw, bhT[s]], [bpsv[p2]])
                    k.copy("act" if half else "dve", vblk[s][:, jt, half * 512:(half + 1) * 512], psv[p2][:], [bpsv[p2]], [bvb[s]])
            k.dma("sp", qT_d[sq, :, :, t0:t0 + N].rearrange("h p t -> p h t"), qblk[s][:], bqb[s], reads=[bqb[s]], writes=[bq_d[b]])
            k.dma("sp", kT_d[sq, :, :, t0:t0 + N].rearrange("h p t -> p h t"), kblk[s][:], bkb[s], reads=[bkb[s]], writes=[bq_d[b]])
            k.dma("sp", v_d[sq, t0:t0 + N, :].rearrange("(j p) e -> p j e", p=128), vblk[s][:], bvb[s], reads=[bvb[s]], writes=[bq_d[b]])
        k.barrier()
    with ExitStack() as pes:
        k.pes = pes
        QB = 256
        lamt = k.sb([128, 256], F32)
        blam = Buf()
        k.dma("sp", lamt[:], g.da_lambda[j:j + 1].rearrange("o a b -> o (a b)").broadcast_to([128, 256]), blam, writes=[blam])
        lprod = k.sb([128, 2, 64], F32)
        k.tt("dve", lprod[:, 0, :], lamt[:, 0:64], lamt[:, 64:128], ALU.mult, [blam], [blam])
        k.tt("dve", lprod[:, 1, :], lamt[:, 128:192], lamt[:, 192:256], ALU.mult, [blam], [blam])
        lsum = k.sb([128, 2], F32)
        k.op("dve", lambda e: e.tensor_reduce(lsum[:], lprod[:], AX.X, ALU.add), [blam], [blam])
        lexp = k.sb([128, 2], F32)
        k.act(lexp[:], lsum[:], AF.Exp, [blam], [blam])
        negl = k.sb([128, 1], F32)
        k.tt("dve", negl[:], lexp[:, 1:2], lexp[:, 0:1], ALU.subtract, [blam], [blam])
        k.ts("dve", negl[:], negl[:], -lam_init, None, ALU.add, None, [blam], [blam])
        gsub = k.sb([128, 128], F32)
        k.dma("sp", gsub[:], g.da_subln_g[j:j + 1].broadcast_to([128, 128]), blam, writes=[blam])
        k.ts("dve", gsub[:], gsub[:], 1.0 - lam_init, None, ALU.mult, None, [blam], [blam])
        tri = k.sb([128, 128], BF16)
        io = k.sb([128, 128], I32)
        k.op("pool", lambda e: e.iota(io[:], [[1, 128]], base=0, channel_multiplier=-1), (), [blam])
        k.ts("dve", tri[:], io[:], 0, None, ALU.is_ge, None, [blam], [blam])
        kT = [k.sb([128, T], BF16) for _ in range(2)]
        qT = [k.sb([128, 2, T], BF16) for _ in range(2)]
        va = [k.sb([128, 32, 129], BF16) for _ in range(2)]
        bkv = [Buf(), Buf()]
        for s in range(2):
            k.memset("dve", va[s][:, :, 128:129], 1.0, [bkv[s]])
            k.memset("dve", qT[s][:], 0.0, [bkv[s]])
        NP = 8
        LAG = 3
        pT = [k.sb([128, 2, QB], BF16) for _ in range(NP)]
        bpT = [Buf() for _ in range(NP)]
        NSB = 5
        pss = [k.psum([128, 512], F32) for _ in range(NSB)]
        bpss = [Buf() for _ in range(NSB)]
        accb = [k.psum([128, 512], F32) for _ in range(2)]
        acc = [[accb[c][:, sb * 256:sb * 256 + 129] for sb in range(2)] for c in range(2)]
        bacc_ = [Buf(), Buf()]
        bacc = [[bacc_[0], bacc_[0]], [bacc_[1], bacc_[1]]]
        NF = 3
        osb = [k.sb([128, 2, 2, 129], F32) for _ in range(NF)]
        rr = [k.sb([128, 2, 2], F32) for _ in range(NF)]
        tt_ = [k.sb([128, 2, 2, 128], F32) for _ in range(NF)]
        ds = [k.sb([128, 2, 128], F32) for _ in range(NF)]
        sqt = [k.sb([128, 2, 128], F32) for _ in range(NF)]
        ssv = [k.sb([128, 2, 2], F32) for _ in range(NF)]
        on = [k.sb([128, 2, 128], BF16) for _ in range(NF)]
        bfin = [Buf() for _ in range(NF)]
        mhalf = k.sb([128, 2], F32)
        k.memset("dve", mhalf[:], -0.5, [blam])
        pst = k.psum([128, 2, 128], BF16)
        bpst = Buf()
        oblk = [k.sb([128, QB], BF16) for _ in range(2)]
        bob = [Buf(), Buf()]
        st8 = {"ip": 0, "ifin": 0, "iob": 0}
        dq = []

        def tick():
            for it in dq:
                it[0] -= 1
            while dq and dq[0][0] <= 0:
                dq.pop(0)[1]()

        def fin_a(sq, h, q0):
            f = st8["ifin"] % NF
            st8["ifin"] += 1
            for c in range(2):
                for sb in range(2):
                    k.copy("dve", osb[f][:, c, sb, :], acc[c][sb], [bacc[c][sb]], [bfin[f]])
            k.recip(rr[f][:], osb[f][:, :, :, 128], [bfin[f]], [bfin[f]])
            k.ts("dve", rr[f][:, 1, :], rr[f][:, 1, :], negl[:, 0:1], None, ALU.mult, None, [bfin[f], blam], [bfin[f]])
            k.tt("dve", tt_[f][:], osb[f][:, :, :, 0:128], rr[f][:].unsqueeze(3).broadcast_to([128, 2, 2, 128]), ALU.mult, [bfin[f]], [bfin[f]])
            k.tt("dve", ds[f][:], tt_[f][:, 0], tt_[f][:, 1], ALU.add, [bfin[f]], [bfin[f]])
            k.tt("dve", sqt[f][:], ds[f][:], ds[f][:], ALU.mult, [bfin[f]], [bfin[f]])
            k.op("dve", lambda e: e.tensor_reduce(ssv[f][:, 0, :], sqt[f][:], AX.X, ALU.add), [bfin[f]], [bfin[f]])
            k.ts("dve", ssv[f][:, 0, :], ssv[f][:, 0, :], 1.0 / 128, 1e-5, ALU.mult, ALU.add, [bfin[f]], [bfin[f]])
            dq.append([6, lambda: fin_b(f, sq, h, q0)])

        def fin_b(f, sq, h, q0):
            k.act(ssv[f][:, 1, :], ssv[f][:, 0, :], AF.Sqrt, [bfin[f]], [bfin[f]])
            dq.append([6, lambda: fin_c(f, sq, h, q0)])

        def fin_c(f, sq, h, q0):
            k.recip(ssv[f][:, 1, :], ssv[f][:, 1, :], [bfin[f]], [bfin[f]])
            k.tt("dve", ds[f][:], ds[f][:], ssv[f][:, 1, :].unsqueeze(2).broadcast_to([128, 2, 128]), ALU.mult, [bfin[f]], [bfin[f]])
            k.tt("dve", on[f][:], ds[f][:], gsub[:, None, :].broadcast_to([128, 2, 128]), ALU.mult, [bfin[f], blam], [bfin[f]])
            ob = st8["iob"] % 2
            st8["iob"] += 1
            for sb in range(2):
                k.tr(pst[:, sb, :], on[f][:, sb, :], g.identb[:], [bfin[f], g.bconst], [bpst])
            k.copy("dve", oblk[ob][:], pst[:].rearrange("p a q -> p (a q)"), [bpst], [bob[ob]])
            k.dma("sp", oT_d[sq, h, :, q0:q0 + QB], oblk[ob][:], bob[ob], reads=[bob[ob]], writes=[bo_d[sq][h]])

        pend = []

        def do_pv(it):
            kt, pp, off, n0, hs, qb, last, sq, h = it
            for c in range(2):
                for sb in range(2):
                    if sb * 128 < n0:
                        continue
                    k.mm(acc[c][sb], pT[pp][:, c, sb * 128:(sb + 1) * 128], va[hs][:, kt, :], kt == 0 and sb == 0, kt == 2 * qb + sb,
                         [bpT[pp], bkv[hs]], [bacc[c][sb]], skip_group_check=True)
            if last:
                fin_a(sq, h, qb * QB)

        for sq in range(NS):
            for h in range(8):
                hs = (sq * 8 + h) % 2
                deps = bq_d[sq * (T // N):(sq + 1) * (T // N)]
                k.dma("sp", kT[hs][:], kT_d[sq, h], bkv[hs], reads=deps, writes=[bkv[hs]])
                for c in range(2):
                    k.dma("sp", qT[hs][c * 64:(c + 1) * 64, c, :], qT_d[sq, h, c * 64:(c + 1) * 64, :], bkv[hs], reads=deps, writes=[bkv[hs]])
                k.dma("sp", va[hs][:, :, 0:128], v_d[sq, :, h * 128:(h + 1) * 128].rearrange("(n p) e -> p n e", p=128), bkv[hs],
                      reads=deps, writes=[bkv[hs]])
                for qb in range(T // QB):
                    q0 = qb * QB
                    nkt = 2 * qb + 2
                    for kt in range(nkt):
                        off = kt * 128 - q0
                        n0 = max(off, 0)
                        ip = st8["ip"]
                        st8["ip"] += 1
                        sp_ = ip % NSB
                        pp = ip % NP
                        psv_ = pss[sp_][:].rearrange("p (c q) -> p c q", c=2)
                        for c in range(2):
                            k.mm(psv_[:, c, n0:QB], kT[hs][:, kt * 128:(kt + 1) * 128], qT[hs][:, c, q0 + n0:q0 + QB], True, True,
                                 [bkv[hs]], [bpss[sp_]])
                        k.act(pT[pp][:, :, n0:QB], psv_[:, :, n0:QB], AF.Exp, [bpss[sp_]], [bpT[pp]])
                        if off >= 0:
                            k.tt("pool", pT[pp][:, :, off:off + 128], pT[pp][:, :, off:off + 128], tri[:, None, :].broadcast_to([128, 2, 128]),
                                 ALU.mult, [bpT[pp], blam], [bpT[pp]])
                        pend.append((kt, pp, off, n0, hs, qb, kt == nkt - 1, sq, h))
                        if len(pend) > LAG:
                            do_pv(pend.pop(0))
                        tick()
                        tick()
        while pend:
            do_pv(pend.pop(0))
        while dq:
            dq.pop(0)[1]()
        k.barrier()
    with ExitStack() as pes:
        k.pes = pes
        wo = k.sb([128, 8, D], BF16)
        bwo = Buf()
        k.dma("pool", wo[:], g.da_w_o[j].rearrange("(c p) n -> p c n", p=128), bwo, writes=[bwo])
        out_proj(g, i, wo, bwo, oT_d, bo_d)
        k.barrier()
    k.pes = outer


def out_proj(g, i, wo, bwo, oT_d, bo_d):
    k = g.k
    N = 256
    nb = NT // N
    xt = [k.sb([128, 8, N], F32) for _ in range(2)]
    bxt = [Buf(), Buf()]
    oT = [k.sb([128, 8, N], BF16) for _ in range(2)]
    boT = [Buf(), Buf()]
    pso = [k.psum([128, 512], F32) for _ in range(2)]
    bpso = [Buf(), Buf()]

    def load(b):
        s = b % 2
        sq, t0 = divmod(b * N, T)
        k.dma("sp", xt[s][:], g.xTv[:, :, b * N:(b + 1) * N], bxt[s], reads=[g.bxT[b]], writes=[bxt[s]])
        k.dma("sp", oT[s][:], oT_d[sq, :, :, t0:t0 + N].rearrange("h p t -> p h t"), boT[s], reads=bo_d[sq], writes=[boT[s]])

    load(0)
    for b in range(nb):
        s = b % 2
        sq = (b * N) // T
        if b + 1 < nb:
            load(b + 1)
        for dc in range(8):
            p2 = dc % 2
            for hh in range(8):
                k.mm(pso[p2][:, :N], wo[:, hh, dc * 128:(dc + 1) * 128], oT[s][:, hh, :], hh == 0, hh == 7, [bwo, boT[s]], [bpso[p2]])
            k.stt(xt[s][:, dc, :], pso[p2][:, :N], g.hg[:, i, sq, 1, dc:dc + 1], xt[s][:, dc, :], ALU.mult, ALU.add,
                  [bpso[p2], g.bmod, bxt[s]], [bxt[s]])
        k.dma("sp", g.xTv[:, :, b * N:(b + 1) * N], xt[s][:], bxt[s], reads=[bxt[s]], writes=[g.bxT[b]])


C0 = -math.exp(-0.5)


class TB:
    __slots__ = ("t", "b")

    def __init__(self, t):
        self.t = t
        self.b = Buf()


def phase_rwkv(g, i):
    k = g.k
    nc = g.nc
    j = i // 3
    NCH = NT // 128
    stage_d = nc.dram_tensor("rw_stage", [NCH, 128, 7, D], BF16, kind="Internal").ap()
    g_d = nc.dram_tensor("rw_gate", [NCH, 128, D], F32, kind="Internal").ap()
    outer = k.pes

    def sbt(shape, dt):
        return TB(k.sb(shape, dt))

    with ExitStack() as ph:
        k.pes = ph
        bc_all = sbt([128, NCH, 16], F32)
        gC_all = sbt([128, NCH, 8], F32)
        cst = sbt([128, 6, 128], F32)
        io = k.sb([128, 128], I32)
        k.op("pool", lambda e: e.iota(io[:], [[1, 128]], base=0, channel_multiplier=-1), (), [cst.b])
        k.ts("dve", cst.t[:, 0, :], io[:], 0, None, ALU.is_gt, None, [cst.b], [cst.b])
        k.ts("dve", cst.t[:, 1, :], io[:], 0, None, ALU.is_ge, None, [cst.b], [cst.b])
        k.ts("dve", cst.t[:, 2, :], io[:], 0, None, ALU.is_lt, None, [cst.b], [cst.b])
        k.ts("dve", cst.t[:, 3, :], cst.t[:, 1, :], C0, None, ALU.mult, None, [cst.b], [cst.b])
        k.memset("dve", cst.t[:, 4, :], C0, [cst.b])
        k.copy("dve", cst.t[:, 5, :], g.ident[:], [cst.b, g.bconst], [cst.b])
        banks = [TB(k.psum([128, 512], F32)) for _ in range(7)]
        pstb = TB(k.psum([128, 1024], BF16))
        bi = [0]

        def bank():
            b = banks[bi[0] % 7]
            bi[0] += 1
            return b

        with ExitStack() as pes:
            k.pes = pes
            wrkv = sbt([128, 3, 8, D], BF16)
            for n in range(3):
                for q in range(2):
                    k.dma("pool", wrkv.t[:, n, q * 4:(q + 1) * 4, :], g.rw_w_rkv[j, n].rearrange("(c p) n -> p c n", p=128)[:, q * 4:(q + 1) * 4, :],
                          wrkv.b, writes=[wrkv.b])
            wl1 = sbt([128, 2, 8, 64], BF16)
            k.dma("pool", wl1.t[:, 0], g.rw_decay_w1[j].rearrange("(c p) n -> p c n", p=128), wl1.b, writes=[wl1.b])
            k.dma("pool", wl1.t[:, 1], g.rw_iclr_a1[j].rearrange("(c p) n -> p c n", p=128), wl1.b, writes=[wl1.b])
            wg1 = sbt([128, 8, 160], BF16)
            k.dma("pool", wg1.t[:], g.rw_gate_g1[j].rearrange("(c p) n -> p c n", p=128), wg1.b, writes=[wg1.b])
            wl2 = sbt([64, 2, D], BF16)
            k.dma("pool", wl2.t[:, 0, :], g.rw_decay_w2[j], wl2.b, writes=[wl2.b])
            k.dma("pool", wl2.t[:, 1, :], g.rw_iclr_a2[j], wl2.b, writes=[wl2.b])
            wg2 = sbt([128, 2, D], BF16)
            k.dma("pool", wg2.t[:, 0, :], g.rw_gate_g2[j, 0:128, :], wg2.b, writes=[wg2.b])
            k.memset("dve", wg2.t[:, 1, :], 0.0, [wg2.b])
            k.dma("pool", wg2.t[0:32, 1, :], g.rw_gate_g2[j, 128:160, :], wg2.b, writes=[wg2.b])
            bct = sbt([128, 5, D], F32)
            for q, src in enumerate((g.rw_decay_w0[j:j + 1], g.rw_iclr_a0[j:j + 1], g.rw_k_k[j:j + 1], g.rw_k_a[j:j + 1],
                                     g.rw_r_k[j:j + 1].rearrange("o h d -> o (h d)"))):
                k.dma("sp", bct.t[:, q, :], src.broadcast_to([128, D]), bct.b, writes=[bct.b])
            muT = sbt([128, 6, 8], F32)
            murow = sbt([48, 128], F32)
            k.dma("sp", murow.t[:], g.rw_mu[j].rearrange("n (c p) -> (n c) p", p=128), murow.b, writes=[murow.b])
            pb = bank()
            k.tr(pb.t[:, 0:48], murow.t[:], g.ident[:48, :48], [murow.b, g.bconst], [pb.b])
            k.copy("dve", muT.t[:].rearrange("p n c -> p (n c)"), pb.t[:, 0:48], [pb.b], [muT.b])
            c0col = sbt([128, 2], F32)
            k.memset("dve", c0col.t[:], C0, [c0col.b])
            xt = [sbt([128, 8, 128], F32) for _ in range(2)]
            hT = [sbt([128, 8, 129], F32) for _ in range(2)]
            hlast = sbt([128, 8, 1], F32)
            xx = sbt([128, 8, 128], F32)
            mixed = [sbt([128, 8, 128], BF16) for _ in range(2)]
            lor = [sbt([128, 128], BF16) for _ in range(2)]
            lor2 = sbt([128, 128], BF16)
            k.memset("dve", lor2.t[:], 0.0, [lor2.b])
            r32_, k32_, a32_, lw_ = [[sbt([128, D], F32) for _ in range(2)] for _ in range(4)]
            cl32, kk32, E1, E2 = [sbt([128, D], F32) for _ in range(4)]
            q1 = []

            def drain1(nops):
                for _ in range(nops):
                    if q1:
                        q1.pop(0)()
            g32 = [sbt([128, D], F32) for _ in range(2)]
            stage = [sbt([128, 7, D], BF16) for _ in range(2)]
            small = sbt([128, 3, 16], F32)
            sqb = [sbt([128, 128], BF16) for _ in range(2)]
            rstd = sbt([128, 2, 128], F32)
            tmpn = [sbt([128, 128], F32) for _ in range(2)]

            def v3(ap):
                return ap.rearrange("p (h d) -> p h d", d=64)

            def load1(ci):
                s = ci % 2
                k.dma("sp", xt[s].t[:], g.xTv[:, :, ci * 128:(ci + 1) * 128], xt[s].b, reads=[g.bxT[ci // 2]], writes=[xt[s].b])

            load1(0)
            for ci in range(NCH):
                s = ci % 2
                sq, n = divmod(ci, NCH // NS)
                if ci + 1 < NCH:
                    load1(ci + 1)
                X, H, SG = xt[s], hT[s], stage[s]
                r32, k32, a32, lw = r32_[s], k32_[s], a32_[s], lw_[s]
                pn = bank()
                for c in range(8):
                    k.act(sqb[c % 2].t[:], X.t[:, c, :], AF.Square, [X.b], [sqb[c % 2].b])
                    k.mm(pn.t[:, :128], g.onesb[:], sqb[c % 2].t[:], c == 0, c == 7, [sqb[c % 2].b, g.bconst], [pn.b])
                k.act(rstd.t[:, 0, :], pn.t[:, :128], AF.Sqrt, [pn.b, g.bconst], [rstd.b], bias=g.epsc[:, 0:1], scale=1.0 / D)
                k.recip(rstd.t[:, 1, :], rstd.t[:, 0, :], [rstd.b], [rstd.b])
                if n == 0:
                    k.memset("dve", H.t[:, :, 0:1], 0.0, [H.b])
                else:
                    k.copy("dve", H.t[:, :, 0:1], hlast.t[:], [hlast.b], [H.b])
                for c in range(8):
                    k.stt(tmpn[c % 2].t[:], X.t[:, c, :], g.gm[:, i, sq, 1, c:c + 1], rstd.t[:, 1, :], ALU.mult, ALU.mult,
                          [X.b, rstd.b, g.bmod], [tmpn[c % 2].b])
                    k.act(H.t[:, c, 1:129], tmpn[c % 2].t[:], AF.Identity, [tmpn[c % 2].b, g.bmod], [H.b],
                          bias=g.modT[:, i, sq, 24 + c:24 + c + 1])
                k.copy("dve", hlast.t[:], H.t[:, :, 128:129], [H.b], [hlast.b])
                k.tt("dve", xx.t[:], H.t[:, :, 0:128], H.t[:, :, 1:129], ALU.subtract, [H.b], [xx.b])
                for n6 in range(6):
                    M = mixed[n6 % 2]
                    for c in range(8):
                        k.stt(M.t[:, c, :], xx.t[:, c, :], muT.t[:, n6, c:c + 1], H.t[:, c, 1:129], ALU.mult, ALU.add,
                              [xx.b, muT.b, H.b], [M.b])
                    if n6 < 3:
                        for half in range(2):
                            pb = bank()
                            for c in range(8):
                                k.mm(pb.t[:], M.t[:, c, :], wrkv.t[:, n6, c, half * 512:(half + 1) * 512], c == 0, c == 7,
                                     [M.b, wrkv.b], [pb.b])
                            hs = slice(half * 512, (half + 1) * 512)
                            if n6 == 0:
                                k.copy("act", r32.t[:, hs], pb.t[:], [pb.b], [r32.b])
                            elif n6 == 1:
                                k.copy("act", k32.t[:, hs], pb.t[:], [pb.b], [k32.b])
                            else:
                                k.copy("act", SG.t[:, 6, hs], pb.t[:], [pb.b], [SG.b])
                    elif n6 < 5:
                        li = n6 - 3
                        L = lor[li]
                        pb = bank()
                        for c in range(8):
                            k.mm(pb.t[0:64, 0:128], wl1.t[:, li, c, :], M.t[:, c, :], c == 0, c == 7, [M.b, wl1.b], [pb.b])
                        k.act(L.t[0:64, :], pb.t[0:64, 0:128], AF.Tanh if li == 0 else AF.Identity, [pb.b], [L.b])
                        dst = lw if li == 0 else a32
                        for half in range(2):
                            hs = slice(half * 512, (half + 1) * 512)
                            pb2 = bank()
                            k.mm(pb2.t[:], L.t[0:64, :], wl2.t[0:64, li, hs], True, True, [L.b, wl2.b], [pb2.b])
                            k.tt("dve", dst.t[:, hs], pb2.t[:], bct.t[:, li, hs], ALU.add, [pb2.b, bct.b], [dst.b])
                        k.act(dst.t[:], dst.t[:], AF.Sigmoid, [dst.b], [dst.b])
                    else:
                        L = lor[0]
                        pb = bank()
                        for c in range(8):
                            k.mm(pb.t[:, 0:128], wg1.t[:, c, 0:128], M.t[:, c, :], c == 0, c == 7, [M.b, wg1.b], [pb.b])
                        k.act(L.t[:], pb.t[:, 0:128], AF.Sigmoid, [pb.b], [L.b])
                        pb = bank()
                        for c in range(8):
                            k.mm(pb.t[0:32, 0:128], wg1.t[:, c, 128:160], M.t[:, c, :], c == 0, c == 7, [M.b, wg1.b], [pb.b])
                        k.act(lor2.t[0:32, :], pb.t[0:32, 0:128], AF.Sigmoid, [pb.b], [lor2.b])
                        G = g32[s]
                        for half in range(2):
                            hs = slice(half * 512, (half + 1) * 512)
                            pb2 = bank()
                            k.mm(pb2.t[:], L.t[:], wg2.t[:, 0, hs], True, False, [L.b, wg2.b], [pb2.b])
                            k.mm(pb2.t[:], lor2.t[:], wg2.t[:, 1, hs], False, True, [lor2.b, wg2.b], [pb2.b])
                            k.copy("act", G.t[:, hs], pb2.t[:], [pb2.b], [G.b])
                        k.dma("sp", g_d[ci], G.t[:], G.b, reads=[G.b])
                def mk_q(ci=ci, r32=r32, k32=k32, a32=a32, lw=lw, SG=SG):
                    ops = []
                    A = ops.append
                    A(lambda: k.tt("dve", kk32.t[:], k32.t[:], bct.t[:, 2, :], ALU.mult, [k32.b, bct.b], [kk32.b]))
                    A(lambda: k.tt("pool", E1.t[:], kk32.t[:], kk32.t[:], ALU.mult, [kk32.b], [E1.b]))
                    A(lambda: k.op("dve", lambda e: e.tensor_reduce(small.t[:, 0, :], v3(E1.t[:]), AX.X, ALU.add), [E1.b], [small.b]))
                    A(lambda: k.act(small.t[:, 0, :], small.t[:, 0, :], AF.Sqrt, [small.b], [small.b]))
                    A(lambda: k.stt(E1.t[:], a32.t[:], -1.0, bct.t[:, 3, :], ALU.add, ALU.mult, [a32.b, bct.b], [E1.b]))
                    A(lambda: k.stt(k32.t[:], E1.t[:], 1.0, k32.t[:], ALU.add, ALU.mult, [E1.b, k32.b], [k32.b]))
                    A(lambda: k.ts("dve", small.t[:, 0, :], small.t[:, 0, :], 1e-12, None, ALU.max, None, [small.b], [small.b]))
                    A(lambda: k.recip(small.t[:, 0, :], small.t[:, 0, :], [small.b], [small.b]))
                    A(lambda: k.tt("dve", v3(kk32.t[:]), v3(kk32.t[:]), small.t[:, 0, :].unsqueeze(2).broadcast_to([128, 16, 64]), ALU.mult,
                                   [small.b, kk32.b], [kk32.b]))
                    A(lambda: k.tt("pool", a32.t[:], a32.t[:], kk32.t[:], ALU.mult, [a32.b, kk32.b], [a32.b]))
                    A(lambda: k.tt("dve", E1.t[:], r32.t[:], k32.t[:], ALU.mult, [r32.b, k32.b], [E1.b]))
                    A(lambda: k.tt("pool", E1.t[:], E1.t[:], bct.t[:, 4, :], ALU.mult, [E1.b, bct.b], [E1.b]))
                    A(lambda: k.op("dve", lambda e: e.tensor_reduce(bc_all.t[:, ci, :], v3(E1.t[:]), AX.X, ALU.add), [E1.b], [bc_all.b]))

                    def cum(half):
                        hs = slice(half * 512, (half + 1) * 512)
                        pa = bank()
                        k.mm(pa.t[:], cst.t[:, 3, :], lw.t[:, hs], True, True, [cst.b, lw.b], [pa.b])
                        k.copy("act", cl32.t[:, hs], pa.t[:], [pa.b], [cl32.b])
                        pb = bank()
                        k.mm(pb.t[:], cst.t[:, 4, :], lw.t[:, hs], True, True, [cst.b, lw.b], [pb.b])
                        k.tt("dve", E2.t[:, hs], pb.t[:], cl32.t[:, hs], ALU.subtract, [pb.b, cl32.b], [E2.b])
                    A(lambda: cum(0))
                    A(lambda: cum(1))

                    def gc():
                        pb = bank()
                        for hp in range(8):
                            k.mm(pb.t[:, 2 * hp:2 * hp + 2], lw.t[:, hp * 128:(hp + 1) * 128], c0col.t[:], True, True, [lw.b, c0col.b], [pb.b])
                        k.act(gC_all.t[:, ci, :], pb.t[:, 0:16:2], AF.Exp, [pb.b], [gC_all.b])
                    A(gc)
                    A(lambda: k.act(E1.t[:], cl32.t[:], AF.Exp, [cl32.b], [E1.b]))
                    A(lambda: k.tt("dve", SG.t[:, 0, :], r32.t[:], E1.t[:], ALU.mult, [r32.b, E1.b], [SG.b]))
                    A(lambda: k.act(E1.t[:], cl32.t[:], AF.Exp, [cl32.b], [E1.b], scale=-1.0))
                    A(lambda: k.tt("dve", SG.t[:, 2, :], a32.t[:], E1.t[:], ALU.mult, [a32.b, E1.b], [SG.b]))
                    A(lambda: k.tt("pool", SG.t[:, 3, :], k32.t[:], E1.t[:], ALU.mult, [k32.b, E1.b], [SG.b]))
                    A(lambda: k.act(E1.t[:], E2.t[:], AF.Exp, [E2.b], [E1.b]))
                    A(lambda: k.tt("dve", SG.t[:, 4, :], a32.t[:], E1.t[:], ALU.mult, [a32.b, E1.b], [SG.b]))
                    A(lambda: k.tt("pool", SG.t[:, 5, :], k32.t[:], E1.t[:], ALU.mult, [k32.b, E1.b], [SG.b]))
                    A(lambda: k.stt(E2.t[:], lw.t[:], -C0, cl32.t[:], ALU.mult, ALU.add, [lw.b, cl32.b, E2.b], [E2.b]))
                    A(lambda: k.act(E1.t[:], E2.t[:], AF.Exp, [E2.b], [E1.b]))
                    A(lambda: k.stt(SG.t[:, 1, :], kk32.t[:], -1.0, E1.t[:], ALU.mult, ALU.mult, [kk32.b, E1.b], [SG.b]))
                    A(lambda: k.dma("sp", stage_d[ci], SG.t[:], SG.b, reads=[SG.b]))
                    return ops

                q1.extend(mk_q())
                while q1:
                    q1.pop(0)()
            k.barrier()
        with ExitStack() as pes:
            if RW_STOP == 1:
                k.pes = outer
                return
            k.pes = pes
            wo = sbt([128, 8, D], BF16)
            k.dma("pool", wo.t[:], g.rw_w_o[j].rearrange("(c p) n -> p c n", p=128), wo.b, writes=[wo.b])
            lnb = sbt([128, 2, D], F32)
            k.dma("sp", lnb.t[:, 0, :], g.rw_lnx_g[j:j + 1].broadcast_to([128, D]), lnb.b, writes=[lnb.b])
            k.dma("sp", lnb.t[:, 1, :], g.rw_lnx_b[j:j + 1].broadcast_to([128, D]), lnb.b, writes=[lnb.b])
            ST = [sbt([128, 8, 128], F32) for _ in range(NS)]
            STb = [sbt([128, 8, 128], BF16) for _ in range(NS)]
            for sq in range(NS):
                k.memset("dve", ST[sq].t[:], 0.0, [ST[sq].b])
                k.memset("dve", STb[sq].t[:], 0.0, [STb[sq].b])
            stage = [sbt([128, 7, D], BF16) for _ in range(2)]
            g32 = [sbt([128, D], F32) for _ in range(2)]
            xt = [sbt([128, 8, 128], F32) for _ in range(2)]
            aT, rT, bT, kT = [sbt([128, 8, 128], BF16) for _ in range(4)]
            arP = sbt([128, 8, 2, 2, 128], BF16)
            bP = sbt([128, 8, 2, 128], BF16)
            k.memset("dve", arP.t[:], 0.0, [arP.b])
            k.memset("dve", bP.t[:], 0.0, [bP.b])
            Y = sbt([128, 16, 128], F32)
            Pg = [[sbt([128, 4, 128], F32) for _ in range(2)] for _ in range(4)]
            PTg = [[sbt([128, 4, 128], F32) for _ in range(2)] for _ in range(4)]
            MNk = sbt([128, 8, 2, 2, 128], BF16)
            Nbr = sbt([128, 16, 128], BF16)
            WT = sbt([128, D], F32)
            UTb = sbt([128, D], BF16)
            o32, E1, E2 = [sbt([128, D], F32) for _ in range(3)]
            og = sbt([128, D], BF16)
            ogT = sbt([128, 8, 128], BF16)
            small = sbt([128, 4, 16], F32)
            ident4 = cst.t[:, 5:6, :].broadcast_to([128, 4, 128])

            def v3(ap):
                return ap.rearrange("p (h d) -> p h d", d=64)

            def load2a(idx):
                s = idx % 2
                n, sq = divmod(idx, NS)
                ci = sq * (NCH // NS) + n
                k.dma("sp", stage[s].t[:], stage_d[ci], stage[s].b, writes=[stage[s].b])

            def load2b(idx):
                s = idx % 2
                n, sq = divmod(idx, NS)
                ci = sq * (NCH // NS) + n
                k.dma("sp", g32[s].t[:], g_d[ci], g32[s].b, writes=[g32[s].b])
                k.dma("sp", xt[s].t[:], g.xTv[:, :, ci * 128:(ci + 1) * 128], xt[s].b, reads=[g.bxT[ci // 2]], writes=[xt[s].b])

            load2a(0)
            load2b(0)
            cq = []
            pe_prev = [None]
            for idx in range(NCH):
                s = idx % 2
                n, sq = divmod(idx, NS)
                ci = sq * (NCH // NS) + n
                SG, G, X = stage[s], g32[s], xt[s]
                S_, Sb = ST[sq], STb[sq]
                for (src, dstT) in ((1, aT), (0, rT), (2, bT), (3, kT)):
                    pb = pstb
                    pv = pb.t[:]
                    for hp in range(8):
                        k.tr(pv[:, hp * 128:(hp + 1) * 128], SG.t[:, src, hp * 128:(hp + 1) * 128], g.identb[:], [SG.b, g.bconst], [pb.b])
                    pv3 = pv.rearrange("p (c t) -> p c t", t=128)
                    k.copy("act", dstT.t[:], pv3, [pb.b], [dstT.b])
                    if src in (0, 1):
                        w = 0 if src == 1 else 1
                        k.copy("pool", arP.t[0:64, :, 0, w, :], dstT.t[0:64], [dstT.b], [arP.b])
                        k.copy("pool", arP.t[64:128, :, 1, w, :], dstT.t[64:128], [dstT.b], [arP.b])
                    elif src == 2:
                        k.copy("pool", bP.t[0:64, :, 0, :], dstT.t[0:64], [dstT.b], [bP.b])
                        k.copy("pool", bP.t[64:128, :, 1, :], dstT.t[64:128], [dstT.b], [bP.b])
                for hp in range(8):
                    grp, pi = divmod(hp, 2)
                    P0, PT0 = Pg[grp][0], PTg[grp][0]
                    pb = bank()
                    k.mm(pb.t[:], bT.t[:, hp, :], arP.t[:, hp].rearrange("p a w t -> p (a w t)"), True, True, [bT.b, arP.b], [pb.b])
                    bv = pb.t[:].rearrange("p (a w t) -> p a w t", a=2, w=2)
                    k.tt("dve", P0.t[:, 2 * pi:2 * pi + 2, :], bv[:, :, 0, :], cst.t[:, 0:1, :].broadcast_to([128, 2, 128]), ALU.mult,
                         [pb.b, cst.b], [P0.b])
                    k.tt("dve", Nbr.t[:, 2 * hp:2 * hp + 2, :], bv[:, :, 1, :], cst.t[:, 1:2, :].broadcast_to([128, 2, 128]), ALU.mult,
                         [pb.b, cst.b], [Nbr.b])
                    pb = bank()
                    k.mm(pb.t[:], kT.t[:, hp, :], arP.t[:, hp].rearrange("p a w t -> p (a w t)"), True, True, [kT.b, arP.b], [pb.b])
                    bv = pb.t[:].rearrange("p (a w t) -> p a w t", a=2, w=2)
                    k.tt("dve", MNk.t[:, hp], bv, cst.t[:, None, 0:2, :].broadcast_to([128, 2, 2, 128]), ALU.mult, [pb.b, cst.b], [MNk.b])
                    pb = bank()
                    k.mm(pb.t[:, 0:256], aT.t[:, hp, :], bP.t[:, hp].rearrange("p a t -> p (a t)"), True, True, [aT.b, bP.b], [pb.b])
                    k.tt("dve", PT0.t[:, 2 * pi:2 * pi + 2, :], pb.t[:, 0:256].rearrange("p (a t) -> p a t", a=2),
                         cst.t[:, 2:3, :].broadcast_to([128, 2, 128]), ALU.mult, [pb.b, cst.b], [PT0.b])
                def drain(nops):
                    for _ in range(nops):
                        if cq:
                            cq.pop(0)()

                for grp in range(4):
                    k.tt("dve", Y.t[:, 4 * grp:4 * grp + 4, :], Pg[grp][0].t[:], ident4, ALU.add, [Pg[grp][0].b, cst.b], [Y.b])
                drain(3)
                for lev in range(1, 7):
                    cur, nxt = (lev - 1) % 2, lev % 2
                    for grp in range(4):
                        Pc, PTc, Pn, PTn = Pg[grp][cur], PTg[grp][cur], Pg[grp][nxt], PTg[grp][nxt]
                        if lev < 6:
                            pb = bank()
                            for q in range(4):
                                k.mm(pb.t[:, q * 128:(q + 1) * 128], PTc.t[:, q, :], Pc.t[:, q, :], True, True, [PTc.b, Pc.b], [pb.b])
                            k.copy("act", Pn.t[:].rearrange("p a t -> p (a t)"), pb.t[:], [pb.b], [Pn.b])
                        pb = bank()
                        for q in range(4):
                            k.mm(pb.t[:, q * 128:(q + 1) * 128], Pc.t[:, q, :], PTc.t[:, q, :], True, True, [PTc.b, Pc.b], [pb.b])
                        k.copy("act", PTn.t[:].rearrange("p a t -> p (a t)"), pb.t[:], [pb.b], [PTn.b])
                    drain(1)
                    for grp in range(4):
                        PTn = PTg[grp][nxt]
                        pb = bank()
                        for q in range(4):
                            k.mm(pb.t[:, q * 128:(q + 1) * 128], PTn.t[:, q, :], Y.t[:, 4 * grp + q, :], True, True, [PTn.b, Y.b], [pb.b])
                        yv = Y.t[:, 4 * grp:4 * grp + 4, :]
                        k.tt("dve", yv, yv, pb.t[:].rearrange("p (a t) -> p a t", a=4), ALU.add, [pb.b, Y.b], [Y.b])
                    drain(2)
                drain(100)
                if idx + 1 < NCH:
                    load2a(idx + 1)
                if (RW_STOP or 99) < 4:
                    k.dma("sp", g.xTv[:, :, ci * 128:(ci + 1) * 128], X.t[:], X.b, reads=[X.b], writes=[g.bxT[ci // 2]])
                    continue
                for hf in range(2):
                    pb = bank()
                    for pp in range(4):
                        hp = hf * 4 + pp
                        k.mm(pb.t[:, pp * 128:(pp + 1) * 128], aT.t[:, hp, :], Sb.t[:, hp, :], True, False, [aT.b, Sb.b], [pb.b])
                        for h2 in range(2):
                            h = 2 * hp + h2
                            k.mm(pb.t[:, pp * 128 + h2 * 64:pp * 128 + (h2 + 1) * 64], MNk.t[:, hp, h2, 0, :], SG.t[:, 6, h * 64:(h + 1) * 64],
                                 False, h2 == 1, [MNk.b, SG.b], [pb.b])
                    k.copy("act", WT.t[:, hf * 512:(hf + 1) * 512], pb.t[:], [pb.b], [WT.b])
                for hf in range(2):
                    pb = bank()
                    for q in range(8):
                        h = hf * 8 + q
                        k.mm(pb.t[:, q * 64:(q + 1) * 64], Y.t[:, h, :], WT.t[:, h * 64:(h + 1) * 64], True, True, [Y.b, WT.b], [pb.b])
                    k.copy("act", UTb.t[:, hf * 512:(hf + 1) * 512], pb.t[:], [pb.b], [UTb.b])
                for hf in range(2):
                    pb = bank()
                    for pp in range(4):
                        hp = hf * 4 + pp
                        k.mm(pb.t[:, pp * 128:(pp + 1) * 128], rT.t[:, hp, :], Sb.t[:, hp, :], True, False, [rT.b, Sb.b], [pb.b])
                        for h2 in range(2):
                            h = 2 * hp + h2
                            cs = slice(pp * 128 + h2 * 64, pp * 128 + (h2 + 1) * 64)
                            k.mm(pb.t[:, cs], Nbr.t[:, h, :], UTb.t[:, h * 64:(h + 1) * 64], False, False, [Nbr.b, UTb.b], [pb.b])
                            k.mm(pb.t[:, cs], MNk.t[:, hp, h2, 1, :], SG.t[:, 6, h * 64:(h + 1) * 64], False, h2 == 1, [MNk.b, SG.b], [pb.b])
                    k.copy("act", o32.t[:, hf * 512:(hf + 1) * 512], pb.t[:], [pb.b], [o32.b])
                for hf in range(2):
                    pb = bank()
                    for pp in range(4):
                        hp = hf * 4 + pp
                        cs = slice(pp * 128, (pp + 1) * 128)
                        k.mm(pb.t[:, cs], SG.t[:, 4, hp * 128:(hp + 1) * 128], UTb.t[:, hp * 128:(hp + 1) * 128], True, False, [SG.b, UTb.b], [pb.b])
                        k.mm(pb.t[:, cs], SG.t[:, 5, hp * 128:(hp + 1) * 128], SG.t[:, 6, hp * 128:(hp + 1) * 128], False, True, [SG.b], [pb.b])
                    for h2 in range(2):
                        rs = slice(h2 * 64, (h2 + 1) * 64)
                        sv = S_.t[rs, hf * 4:(hf + 1) * 4, h2 * 64:(h2 + 1) * 64]
                        k.tt("dve", sv, sv, gC_all.t[rs, ci, hf * 4:(hf + 1) * 4].unsqueeze(2).broadcast_to([64, 4, 64]), ALU.mult,
                             [S_.b, gC_all.b, Sb.b], [S_.b])
                        k.tt("dve", sv, sv, pb.t[rs, :].rearrange("p (c x) -> p c x", c=4)[:, :, h2 * 64:(h2 + 1) * 64], ALU.add, [pb.b, S_.b], [S_.b])
                        k.copy("dve", Sb.t[rs, hf * 4:(hf + 1) * 4, h2 * 64:(h2 + 1) * 64], sv, [S_.b], [Sb.b])
                def mk_tail(ci=ci, sq=sq, SG=SG, G=G, X=X):
                    ops = []
                    o3 = v3(o32.t[:])
                    ops.append(lambda: k.op("dve", lambda e: e.tensor_reduce(small.t[:, 0, :], o3, AX.X, ALU.add), [o32.b], [small.b]))
                    ops.append(lambda: k.tt("pool", E1.t[:], o32.t[:], o32.t[:], ALU.mult, [o32.b], [E1.b]))
                    ops.append(lambda: k.op("dve", lambda e: e.tensor_reduce(small.t[:, 1, :], v3(E1.t[:]), AX.X, ALU.add), [E1.b], [small.b]))

                    def stats():
                        k.ts("dve", small.t[:, 0, :], small.t[:, 0, :], 1.0 / 64, None, ALU.mult, None, [small.b], [small.b])
                        k.tt("dve", small.t[:, 2, :], small.t[:, 0, :], small.t[:, 0, :], ALU.mult, [small.b], [small.b])
                        k.stt(small.t[:, 1, :], small.t[:, 1, :], 1.0 / 64, small.t[:, 2, :], ALU.mult, ALU.subtract, [small.b], [small.b])
                        k.act(small.t[:, 1, :], small.t[:, 1, :], AF.Sqrt, [small.b, g.bconst], [small.b], bias=g.epsc[:, 2:3])
                    ops.append(stats)
                    ops.append(lambda: k.tt("pool", v3(E2.t[:]), v3(SG.t[:, 6, :]), bc_all.t[:, ci, :].unsqueeze(2).broadcast_to([128, 16, 64]), ALU.mult,
                                            [SG.b, bc_all.b], [E2.b]))
                    ops.append(lambda: k.recip(small.t[:, 1, :], small.t[:, 1, :], [small.b], [small.b]))
                    ops.append(lambda: k.tt("dve", v3(E1.t[:]), o3, small.t[:, 0, :].unsqueeze(2).broadcast_to([128, 16, 64]), ALU.subtract, [o32.b, small.b], [E1.b]))
                    ops.append(lambda: k.tt("dve", v3(E1.t[:]), v3(E1.t[:]), small.t[:, 1, :].unsqueeze(2).broadcast_to([128, 16, 64]), ALU.mult, [E1.b, small.b], [E1.b]))
                    ops.append(lambda: k.tt("pool", E1.t[:], E1.t[:], lnb.t[:, 0, :], ALU.mult, [E1.b, lnb.b], [E1.b]))
                    ops.append(lambda: k.tt("pool", E1.t[:], E1.t[:], lnb.t[:, 1, :], ALU.add, [E1.b, lnb.b], [E1.b]))
                    ops.append(lambda: k.tt("dve", E1.t[:], E1.t[:], E2.t[:], ALU.add, [E1.b, E2.b], [E1.b]))
                    ops.append(lambda: k.tt("dve", og.t[:], E1.t[:], G.t[:], ALU.mult, [E1.b, G.b], [og.b]))

                    def pe_part():
                        pb = pstb
                        pv = pb.t[:]
                        for c in range(8):
                            k.tr(pv[:, c * 128:(c + 1) * 128], og.t[:, c * 128:(c + 1) * 128], g.identb[:], [og.b, g.bconst], [pb.b])
                        k.copy("act", ogT.t[:].rearrange("p c t -> p (c t)"), pv, [pb.b], [ogT.b])
                        for hf in range(2):
                            pb = bank()
                            for q in range(4):
                                dc = hf * 4 + q
                                for c in range(8):
                                    k.mm(pb.t[:, q * 128:(q + 1) * 128], wo.t[:, c, dc * 128:(dc + 1) * 128], ogT.t[:, c, :], c == 0, c == 7,
                                         [wo.b, ogT.b], [pb.b])
                            for q in range(4):
                                dc = hf * 4 + q
                                k.stt(X.t[:, dc, :], pb.t[:, q * 128:(q + 1) * 128], g.hg[:, i, sq, 1, dc:dc + 1], X.t[:, dc, :], ALU.mult, ALU.add,
                                      [pb.b, g.bmod, X.b], [X.b])
                        k.dma("sp", g.xTv[:, :, ci * 128:(ci + 1) * 128], X.t[:], X.b, reads=[X.b], writes=[g.bxT[ci // 2]])
                    return ops, pe_part

                if pe_prev[0] is not None:
                    pe_prev[0]()
                if idx + 1 < NCH:
                    load2b(idx + 1)
                ops, pe_part = mk_tail()
                cq.extend(ops)
                pe_prev[0] = pe_part
                if RW_STOP == 7:
                    while cq:
                        cq.pop(0)()
                    pe_part()
                    pe_prev[0] = None
            while cq:
                cq.pop(0)()
            if pe_prev[0] is not None:
                pe_prev[0]()
            k.barrier()
    k.pes = outer


def phase_swa(g, i):
    k = g.k
    j = i // 3
    N = 256
    nb = NT // N
    wq = k.sb([128, 8, 1024], BF16)
    wkd = k.sb([128, 8, 2, 128], BF16)
    wv = k.sb([128, 8, 128], BF16)
    wo = k.sb([128, 8, D], BF16)
    bw = Buf()
    wsrc = g.sw_w_qkv[j].rearrange("(c p) n -> p c n", p=128)
    k.dma("pool", wq[:], wsrc[:, :, 0:1024], bw, writes=[bw])
    for kv in range(2):
        for r in range(2):
            k.dma("pool", wkd[:, :, kv, r * 64:(r + 1) * 64], wsrc[:, :, 1024 + kv * 64:1024 + (kv + 1) * 64], bw, writes=[bw])
    k.dma("pool", wv[:], wsrc[:, :, 1152:1280], bw, writes=[bw])
    k.dma("pool", wo[:], g.sw_w_o[j].rearrange("(c p) n -> p c n", p=128), bw, writes=[bw])
    gq = k.sb([128, 2], F32)
    bgq = Buf()
    load_vec128(g, gq[:, 0:1], g.sw_q_norm_g[j], bgq)
    load_vec128(g, gq[:, 1:2], g.sw_k_norm_g[j], bgq)
    k.ts("dve", gq[:, 0:1], gq[:, 0:1], 0.125, None, ALU.mult, None, [bgq], [bgq])
    es = k.sb([128, 16], F32)
    k.dma("sp", es[:], g.sw_sinks[j:j + 1].broadcast_to([128, 16]), bgq, writes=[bgq])
    k.act(es[:], es[:], AF.Exp, [bgq], [bgq])
    io = k.sb([128, 128], I32)
    k.op("pool", lambda e: e.iota(io[:], [[1, 128]], base=0, channel_multiplier=-1), (), [bgq])
    masks = k.sb([128, 2, 128], BF16)
    k.ts("dve", masks[:, 0, :], io[:], 0, None, ALU.is_lt, None, [bgq], [bgq])
    k.ts("dve", masks[:, 1, :], io[:], 0, None, ALU.is_ge, None, [bgq], [bgq])
    xt = [k.sb([128, 8, N], F32) for _ in range(2)]
    bxt = [Buf(), Buf()]
    hT = k.sb([128, 8, N], BF16)
    bhT = Buf()
    qT = k.sb([128, 8, N], BF16)
    bqT = Buf()
    kTd = [k.sb([128, 2, N], BF16) for _ in range(2)]
    bkT = [Buf(), Buf()]
    va = [k.sb([128, 2, 2, 65], BF16) for _ in range(2)]
    bva = [Buf(), Buf()]
    for s in range(2):
        k.memset("dve", va[s][:, :, :, 64:65], 1.0, [bva[s]])
    st = norm_scratch(g, N)
    qs = qk_scratch(g, N, nps=0)
    banks = [TB(k.psum([128, 512], F32)) for _ in range(6)]
    bi = [0]

    def bank():
        bb = banks[bi[0] % 6]
        bi[0] += 1
        return bb

    pst = k.psum([128, 128], BF16)
    bpst = Buf()
    pT = [k.sb([128, 4, 128], BF16) for _ in range(4)]
    bpT = [Buf() for _ in range(4)]
    l4 = [k.sb([128, 4], F32) for _ in range(2)]
    bl4 = [Buf(), Buf()]
    otok = k.sb([128, D], BF16)
    botok = Buf()
    oT = k.sb([128, 8, N], BF16)
    boT = Buf()

    def load(b):
        s = b % 2
        k.dma("sp", xt[s][:], g.xTv[:, :, b * N:(b + 1) * N], bxt[s], reads=[g.bxT[b]], writes=[bxt[s]])

    load(0)
    ip = 0
    il = 0
    for b in range(nb):
        s = b % 2
        sq, t0 = divmod(b * N, T)
        if b + 1 < nb:
            load(b + 1)
        norm_mod(g, xt[s], bxt[s], hT, bhT, i, sq, 1, N, st)
        jobs = [(wq[:, :, cq * 128:(cq + 1) * 128], qT[:, cq, :], gq[:, 0:1], bqT) for cq in range(8)]
        jobs += [(wkd[:, :, kv, :], kTd[s][:, kv, :], gq[:, 1:2], bkT[s]) for kv in range(2)]
        nj = len(jobs)
        pj = [None] * nj
        sj = [None] * nj
        for ii in range(nj + 2):
            if ii < nj:
                pj[ii] = bank()
                for c in range(8):
                    k.mm(pj[ii].t[:, :N], jobs[ii][0][:, c, :], hT[:, c, :], c == 0, c == 7, [bw, bhT], [pj[ii].b])
            if 1 <= ii <= nj:
                q_ = ii - 1
                s2 = q_ % 2
                k.act(qs["sq"][s2][:, :N], pj[q_].t[:, :N], AF.Square, [pj[q_].b], [qs["bsq"][s2]])
                sj[q_] = bank()
                k.mm(sj[q_].t[:, :N], g.blk64[:], qs["sq"][s2][:, :N], True, True, [qs["bsq"][s2], g.bconst], [sj[q_].b])
            if ii >= 2:
                q_ = ii - 2
                s2 = q_ % 2
                _, outap, gvec, bdst = jobs[q_]
                k.act(qs["rt"][s2][:, :N], sj[q_].t[:, :N], AF.Sqrt, [sj[q_].b, g.bconst], [qs["brt"][s2]], bias=g.epsc[:, 0:1], scale=1.0 / 64)
                k.recip(qs["rs"][s2][:, :N], qs["rt"][s2][:, :N], [qs["brt"][s2]], [qs["brs"][s2]])
                k.stt(outap, pj[q_].t[:, :N], gvec, qs["rs"][s2][:, :N], ALU.mult, ALU.mult, [pj[q_].b, qs["brs"][s2], bgq], [bdst])
        for jt in range(2):
            pv_ = bank()
            for c in range(8):
                k.mm(pv_.t[:, 0:128], hT[:, c, jt * 128:(jt + 1) * 128], wv[:, c, :], c == 0, c == 7, [bw, bhT], [pv_.b])
            k.copy("dve", va[s][:, jt, :, 0:64], pv_.t[:, 0:128].rearrange("p (a d) -> p a d", a=2), [pv_.b], [bva[s]])
        for jt in range(2):
            tiles = []
            if jt == 1:
                tiles.append((s, 0, 0))
            elif t0 > 0:
                tiles.append((1 - s, 1, 0))
            tiles.append((s, jt, 1))
            for kv in range(2):
                for half in range(2):
                    pts = []
                    for (ks, ktile, mi) in tiles:
                        psb = bank()
                        pp = ip % 4
                        ip += 1
                        for hh in range(4):
                            hq = kv * 8 + half + 2 * hh
                            r0 = (hq % 2) * 64
                            k.mm(psb.t[:, hh * 128:(hh + 1) * 128], kTd[ks][r0:r0 + 64, kv, ktile * 128:(ktile + 1) * 128],
                                 qT[r0:r0 + 64, hq // 2, jt * 128:(jt + 1) * 128], True, True, [bkT[ks], bqT], [psb.b])
                        k.act(pT[pp][:].rearrange("p a q -> p (a q)"), psb.t[:], AF.Exp, [psb.b], [bpT[pp]])
                        k.tt("pool", pT[pp][:], pT[pp][:], masks[:, mi:mi + 1, :].broadcast_to([128, 4, 128]), ALU.mult,
                             [bpT[pp], bgq], [bpT[pp]])
                        pts.append((pp, ks, ktile))
                    accb = bank()
                    acc, bacc = accb.t, accb.b
                    for hh in range(4):
                        for ti, (pp, ks, ktile) in enumerate(pts):
                            k.mm(acc[:, hh * 65:(hh + 1) * 65], pT[pp][:, hh, :], va[ks][:, ktile, kv, :], ti == 0, ti == len(pts) - 1,
                                 [bpT[pp], bva[ks]], [bacc])
                    li = il % 2
                    il += 1
                    h0 = kv * 8 + half
                    accv = acc[:, 0:260].rearrange("p (a e) -> p a e", e=65)
                    k.tt("dve", l4[li][:], accv[:, :, 64], es[:, h0:h0 + 7:2], ALU.add, [bacc, bgq], [bl4[li]])
                    k.recip(l4[li][:], l4[li][:], [bl4[li]], [bl4[li]])
                    k.tt("dve", otok[:].rearrange("p (a d) -> p a d", d=64)[:, h0:h0 + 7:2, :], accv[:, :, 0:64],
                         l4[li][:].unsqueeze(2).broadcast_to([128, 4, 64]), ALU.mult, [bacc, bl4[li]], [botok])
            for c in range(8):
                k.tr(pst[:], otok[:, c * 128:(c + 1) * 128], g.identb[:], [botok, g.bconst], [bpst])
                k.copy("act", oT[:, c, jt * 128:(jt + 1) * 128], pst[:], [bpst], [boT])
        for dc in range(8):
            pob = bank()
            for c in range(8):
                k.mm(pob.t[:, :N], wo[:, c, dc * 128:(dc + 1) * 128], oT[:, c, :], c == 0, c == 7, [bw, boT], [pob.b])
            k.stt(xt[s][:, dc, :], pob.t[:, :N], g.hg[:, i, sq, 1, dc:dc + 1], xt[s][:, dc, :], ALU.mult, ALU.add,
                  [pob.b, g.bmod, bxt[s]], [bxt[s]])
        k.dma("sp", g.xTv[:, :, b * N:(b + 1) * N], xt[s][:], bxt[s], reads=[bxt[s]], writes=[g.bxT[b]])


_IN_NAMES = ["norm_g", "ada_w", "ada_b", "ffn_w_in", "ffn_w_out", "da_w_qkv", "da_w_o", "da_q_norm_g", "da_k_norm_g",
             "da_lambda", "da_subln_g", "rw_mu", "rw_w_rkv", "rw_w_o", "rw_decay_w0", "rw_decay_w1", "rw_decay_w2",
             "rw_iclr_a0", "rw_iclr_a1", "rw_iclr_a2", "rw_gate_g1", "rw_gate_g2", "rw_k_k", "rw_k_a", "rw_r_k",
             "rw_lnx_g", "rw_lnx_b", "sw_w_qkv", "sw_w_o", "sw_q_norm_g", "sw_k_norm_g", "sw_sinks"]


def make_in_maps(inputs, ncores=NCORES):
    shared = {n: np.ascontiguousarray(inputs[n], dtype=np.float32) for n in _IN_NAMES}
    maps = []
    for cidx in range(ncores):
        m = dict(shared)
        m["x"] = np.ascontiguousarray(inputs["x"][cidx * NS:(cidx + 1) * NS], dtype=np.float32)
        m["c"] = np.ascontiguousarray(inputs["c"][cidx * NS:(cidx + 1) * NS], dtype=np.float32)
        maps.append(m)
    return maps


def kernel(**inputs):
    nc = build()
    maps = make_in_maps(inputs)
    res = run_bass_kernel_spmd(nc, maps, core_ids=list(range(NCORES)))
    return np.concatenate([r["y"] for r in res.results], axis=0).astype(np.float32)
```

```python
import math
from contextlib import ExitStack

import numpy as np
import concourse.bass as bass
import concourse.mybir as mybir
from concourse.bass_utils import run_bass_kernel_spmd

F32 = mybir.dt.float32
BF16 = mybir.dt.bfloat16
I32 = mybir.dt.int32
AF = mybir.ActivationFunctionType
ALU = mybir.AluOpType
AX = mybir.AxisListType

NCORES = 8
NS = 2
T = 4096
NT = NS * T
D = 1024
DC = 8
DFF = 2816
FC = 22
DEPTH = 4
EPS = 1e-6
DBG_EXT = False
RW_STOP = 0
USE_ARS = False


class Buf:
    __slots__ = ("w", "r", "rec", "name")

    def __init__(self, name=""):
        self.w = None
        self.r = {}
        self.rec = None
        self.name = name


class K:
    def __init__(self, nc, es, same_engine_sync=False):
        self.nc = nc
        self.es = es
        self.pes = es
        self.eng = {"pe": nc.tensor, "act": nc.scalar, "dve": nc.vector, "pool": nc.gpsimd, "sp": nc.sync}
        self.esem = {e: es.enter_context(nc.semaphore("s_" + e)) for e in ("pe", "act", "dve", "pool")}
        self.ecnt = {e: 0 for e in self.esem}
        self.waited = {e: {} for e in self.eng}
        self.same = same_engine_sync
        self.recs = []
        self.free_recs = []
        self.phase_recs = []
        self.nalloc = 0

    def sb(self, shape, dt, name=None):
        self.nalloc += 1
        return self.pes.enter_context(self.nc.sbuf_tensor(name or ("t%d" % self.nalloc), list(shape), dt))

    def psum(self, shape, dt, name=None):
        self.nalloc += 1
        return self.pes.enter_context(self.nc.psum_tensor(name or ("p%d" % self.nalloc), list(shape), dt))

    def _wait(self, e, tok):
        if tok is None:
            return
        key, sem, val, owner = tok
        if owner == e and (e == "pe" or not self.same):
            return
        d = self.waited[e]
        if d.get(key, 0) >= val:
            return
        self.eng[e].wait_ge(sem, val)
        d[key] = val

    def _deps(self, e, reads, writes):
        for b in reads:
            self._wait(e, b.w)
        for b in writes:
            self._wait(e, b.w)
            for t in b.r.values():
                self._wait(e, t)

    def _mark(self, tok, reads, writes):
        for b in reads:
            b.r[tok[0]] = tok
        for b in writes:
            b.w = tok
            b.r = {}

    def op(self, e, ins_fn, reads=(), writes=()):
        self._deps(e, reads, writes)
        ins = ins_fn(self.eng[e])
        self.ecnt[e] += 1
        ins.then_inc(self.esem[e], 1)
        tok = (e, self.esem[e], self.ecnt[e], e)
        self._mark(tok, reads, writes)
        return tok

    def dma(self, q, out, in_, owner, reads=(), writes=(), **kw):
        self._deps(q, reads, writes)
        if owner.rec is None:
            if self.free_recs:
                owner.rec = self.free_recs.pop()
            else:
                owner.rec = [self.es.enter_context(self.nc.semaphore("d%d" % len(self.recs))), 0, len(self.recs)]
                self.recs.append(owner.rec)
            self.phase_recs.append(owner.rec)
        rec = owner.rec
        self.eng[q].dma_start(out=out, in_=in_, **kw).then_inc(rec[0], 16)
        rec[1] += 16
        tok = ("d%d" % rec[2], rec[0], rec[1], "dma")
        self._mark(tok, reads, writes)
        return tok

    def barrier(self):
        for e in self.eng:
            for o in self.esem:
                if o != e and self.ecnt[o] > 0:
                    self._wait(e, (o, self.esem[o], self.ecnt[o], o))
            for rec in self.recs:
                if rec[1] > 0:
                    self._wait(e, ("d%d" % rec[2], rec[0], rec[1], "dma"))
        self.free_recs.extend(self.phase_recs)
        self.phase_recs = []

    def mm(self, out, lhsT, rhs, start, stop, reads, writes, **kw):
        return self.op("pe", lambda e: e.matmul(out, lhsT, rhs, start=start, stop=stop, **kw), reads, writes)

    def tr(self, out, in_, ident, reads, writes):
        return self.op("pe", lambda e: e.transpose(out, in_, ident), reads, writes)

    def act(self, out, in_, func, reads, writes, bias=None, scale=None, eng="act"):
        kw = {}
        if bias is not None:
            kw["bias"] = bias
        if scale is not None:
            kw["scale"] = scale
        return self.op(eng, lambda e: e.activation(out, in_, func, **kw), reads, writes)

    def ts(self, eng, out, in0, s1, s2, op0, op1, reads, writes):
        if op1 is None:
            return self.op(eng, lambda e: e.tensor_scalar(out, in0, s1, None, op0=op0), reads, writes)
        return self.op(eng, lambda e: e.tensor_scalar(out, in0, s1, s2, op0=op0, op1=op1), reads, writes)

    def tt(self, eng, out, in0, in1, op, reads, writes):
        return self.op(eng, lambda e: e.tensor_tensor(out, in0, in1, op), reads, writes)

    def stt(self, out, in0, scalar, in1, op0, op1, reads, writes):
        return self.op("dve", lambda e: e.scalar_tensor_tensor(out, in0, scalar, in1, op0=op0, op1=op1), reads, writes)

    def copy(self, eng, out, in_, reads, writes):
        if eng == "act":
            return self.op("act", lambda e: e.copy(out, in_), reads, writes)
        return self.op(eng, lambda e: e.tensor_copy(out, in_), reads, writes)

    def recip(self, out, in_, reads, writes):
        return self.op("dve", lambda e: e.reciprocal(out, in_), reads, writes)

    def memset(self, eng, out, val, writes):
        return self.op(eng, lambda e: e.memset(out, val), (), writes)


class Ctx:
    pass


def build(stop_after=None, same_engine_sync=True, dbg_mod=False, phase_list=None):
    nc = bass.Bass("TRN2", target_bir_lowering=False)
    g = Ctx()
    g.nc = nc

    def din(name, shape):
        return nc.dram_tensor(name, list(shape), F32, kind="ExternalInput").ap()

    g.x = din("x", [NS, T, D])
    g.c = din("c", [NS, D])
    g.norm_g = din("norm_g", [DEPTH, 3, D])
    g.ada_w = din("ada_w", [DEPTH, D, 9 * D])
    g.ada_b = din("ada_b", [DEPTH, 9 * D])
    g.ffn_w_in = din("ffn_w_in", [DEPTH, 2, D, 2 * DFF])
    g.ffn_w_out = din("ffn_w_out", [DEPTH, 2, DFF, D])
    g.da_w_qkv = din("da_w_qkv", [2, D, 3 * D])
    g.da_w_o = din("da_w_o", [2, D, D])
    g.da_q_norm_g = din("da_q_norm_g", [2, 64])
    g.da_k_norm_g = din("da_k_norm_g", [2, 64])
    g.da_lambda = din("da_lambda", [2, 4, 64])
    g.da_subln_g = din("da_subln_g", [2, 128])
    g.rw_mu = din("rw_mu", [1, 6, D])
    g.rw_w_rkv = din("rw_w_rkv", [1, 3, D, D])
    g.rw_w_o = din("rw_w_o", [1, D, D])
    g.rw_decay_w0 = din("rw_decay_w0", [1, D])
    g.rw_decay_w1 = din("rw_decay_w1", [1, D, 64])
    g.rw_decay_w2 = din("rw_decay_w2", [1, 64, D])
    g.rw_iclr_a0 = din("rw_iclr_a0", [1, D])
    g.rw_iclr_a1 = din("rw_iclr_a1", [1, D, 64])
    g.rw_iclr_a2 = din("rw_iclr_a2", [1, 64, D])
    g.rw_gate_g1 = din("rw_gate_g1", [1, D, 160])
    g.rw_gate_g2 = din("rw_gate_g2", [1, 160, D])
    g.rw_k_k = din("rw_k_k", [1, D])
    g.rw_k_a = din("rw_k_a", [1, D])
    g.rw_r_k = din("rw_r_k", [1, 16, 64])
    g.rw_lnx_g = din("rw_lnx_g", [1, D])
    g.rw_lnx_b = din("rw_lnx_b", [1, D])
    g.sw_w_qkv = din("sw_w_qkv", [1, D, 1280])
    g.sw_w_o = din("sw_w_o", [1, D, D])
    g.sw_q_norm_g = din("sw_q_norm_g", [1, 64])
    g.sw_k_norm_g = din("sw_k_norm_g", [1, 64])
    g.sw_sinks = din("sw_sinks", [1, 16])
    g.y = nc.dram_tensor("y", [NS, T, D], F32, kind="ExternalOutput").ap()
    g.xT = nc.dram_tensor("xT_scratch", [D, NT], F32, kind="Internal").ap()
    g.xTv = g.xT.rearrange("(c p) t -> p c t", p=128)
    if dbg_mod:
        g.dbg = nc.dram_tensor("dbg", [128, DEPTH * NS * 72], F32, kind="ExternalOutput").ap()

    with ExitStack() as es:
        k = K(nc, es, same_engine_sync)
        g.k = k
        g.bxT = [Buf("xT%d" % i) for i in range(NT // 256)]
        phase_consts(g)
        phases = [("tin", None)]
        for i in range(DEPTH):
            phases.append(("ffn", (i, 0)))
            phases.append(("mix", i))
            phases.append(("ffn", (i, 1)))
        phases.append(("tout", None))
        if phase_list is not None:
            phases = phase_list
        n = 0
        for kind, arg in phases:
            if stop_after is not None and n > stop_after and kind != "tout":
                continue
            n += 1
            with ExitStack() as pes:
                k.pes = pes
                if kind == "tin":
                    phase_tin(g)
                elif kind == "tout":
                    phase_tout(g)
                elif kind == "ffn":
                    phase_ffn(g, *arg)
                elif kind == "mix":
                    phase_mix(g, arg)
                k.barrier()
            k.pes = es
        if dbg_mod:
            b = Buf()
            k.dma("sp", g.dbg, g.modT[:].rearrange("p a b c -> p (a b c)"), b, reads=[g.bmod], writes=[b])
            k.barrier()
    return nc


def phase_consts(g):
    k = g.k
    nc = g.nc
    g.ident = k.sb([128, 128], F32, "ident")
    g.identb = k.sb([128, 128], BF16, "identb")
    g.onesb = k.sb([128, 128], BF16, "onesb")
    g.bconst = Buf("const")
    g.modT = k.sb([128, DEPTH, NS, 72], F32, "modT")
    g.gm = k.sb([128, DEPTH, NS, 3, 8], F32, "gm")
    g.hg = k.sb([128, DEPTH, NS, 3, 8], F32, "hg")
    g.bmod = Buf("mod")
    g.blk64 = k.sb([128, 128], BF16, "blk64")
    k.memset("dve", g.blk64[:], 0.0, [g.bconst])
    k.memset("dve", g.blk64[0:64, 0:64], 1.0, [g.bconst])
    k.memset("dve", g.blk64[64:128, 64:128], 1.0, [g.bconst])
    g.epsc = k.sb([128, 4], F32, "epsc")
    for q, v in enumerate((EPS, 1e-5, 64e-5, 1e-24)):
        k.memset("dve", g.epsc[:, q:q + 1], v, [g.bconst])
    with ExitStack() as pes:
        k.pes = pes
        io = k.sb([128, 128], I32, "iota")
        bio = Buf()
        k.op("pool", lambda e: e.iota(io[:], [[1, 128]], base=0, channel_multiplier=-1), (), [bio])
        k.ts("dve", g.ident[:], io[:], 0, None, ALU.is_equal, None, [bio], [g.bconst])
        k.copy("dve", g.identb[:], g.ident[:], [g.bconst], [g.bconst])
        k.memset("dve", g.onesb[:], 1.0, [g.bconst])
        csb = k.sb([NS, D], F32)
        bc = Buf()
        k.dma("sp", csb[:], g.c, bc, writes=[bc])
        csl = k.sb([NS, D], F32)
        bcs = Buf()
        k.act(csl[:], csb[:], AF.Silu, [bc], [bcs])
        pst = k.psum([128, 512], F32)
        bps = Buf()
        for c in range(8):
            k.tr(pst[:, c * NS:(c + 1) * NS], csl[:, c * 128:(c + 1) * 128], g.ident[:NS, :NS], [bcs, g.bconst], [bps])
        condT = k.sb([128, 8, NS], F32)
        bcond = Buf()
        k.copy("dve", condT[:].rearrange("p c s -> p (c s)"), pst[:, :8 * NS], [bps], [bcond])
        adabT = k.sb([128, DEPTH * 72], F32)
        ngT = k.sb([128, DEPTH * 3 * 8], F32)
        bab = Buf()
        rows = k.sb([96, 4, 128], F32)
        brow = Buf()
        abv = g.ada_b.rearrange("l (j p) -> (l j) p", p=128)
        for q in range(3):
            k.dma("sp", rows[:, q, :], abv[q * 96:(q + 1) * 96, :], brow, writes=[brow])
        k.dma("sp", rows[:, 3, :], g.norm_g.rearrange("l k (c p) -> (l k c) p", p=128), brow, writes=[brow])
        pst2 = k.psum([128, 512], F32)
        bps2 = Buf()
        for q in range(4):
            k.tr(pst2[:, q * 96:(q + 1) * 96], rows[:, q, :], g.ident[:96, :96], [brow, g.bconst], [bps2])
        k.copy("dve", adabT[:], pst2[:, :288], [bps2], [bab])
        k.copy("dve", ngT[:], pst2[:, 288:384], [bps2], [bab])
        NPC = 8
        PW = 9 * D // NPC
        wbuf = [k.sb([128, 8, PW], F32) for _ in range(2)]
        bwb = [Buf(), Buf()]
        psm = [k.psum([128, 512], F32) for _ in range(2)]
        bpsm = [Buf(), Buf()]
        it = 0
        for i in range(DEPTH):
            wv = g.ada_w[i].rearrange("(c p) n -> p c n", p=128)
            for q in range(NPC):
                s = it % 2
                for half in range(2):
                    k.dma("sp", wbuf[s][:, half * 4:(half + 1) * 4, :], wv[:, half * 4:(half + 1) * 4, q * PW:(q + 1) * PW],
                          bwb[s], writes=[bwb[s]])
                for jj in range(9):
                    for c in range(8):
                        k.mm(psm[s][:, jj * NS:(jj + 1) * NS], wbuf[s][:, c, jj * 128:(jj + 1) * 128], condT[:, c, :],
                             c == 0, c == 7, [bwb[s], bcond], [bpsm[s]])
                for sq in range(NS):
                    k.tt("dve", g.modT[:, i, sq, q * 9:(q + 1) * 9], psm[s][:, sq:9 * NS:NS],
                         adabT[:, i * 72 + q * 9:i * 72 + (q + 1) * 9], ALU.add, [bpsm[s], bab], [g.bmod])
                it += 1
        for i in range(DEPTH):
            for sq in range(NS):
                for n in range(3):
                    k.stt(g.gm[:, i, sq, n, :], g.modT[:, i, sq, (3 * n + 1) * 8:(3 * n + 2) * 8], 1.0,
                          ngT[:, (i * 3 + n) * 8:(i * 3 + n + 1) * 8], ALU.add, ALU.mult, [g.bmod, bab], [g.bmod])
                    k.ts("dve", g.hg[:, i, sq, n, :], g.modT[:, i, sq, (3 * n + 2) * 8:(3 * n + 3) * 8],
                         0.5 if n != 1 else 1.0, None, ALU.mult, None, [g.bmod], [g.bmod])
        k.barrier()
    k.pes = k.es


def shiftv(g, i, sq, n):
    return g.modT[:, i, sq, (3 * n) * 8:(3 * n + 1) * 8]


def phase_tin(g):
    k = g.k
    xtok = [k.sb([128, 2, D], F32) for _ in range(2)]
    bxtok = [Buf(), Buf()]
    xt = [k.sb([128, 8, 256], F32) for _ in range(2)]
    bxt = [Buf(), Buf()]
    ps = [k.psum([128, 512], F32) for _ in range(8)]
    bps = [Buf() for _ in range(8)]
    for b in range(NT // 256):
        s = b % 2
        sq, t0 = divmod(b * 256, T)
        k.dma("sp", xtok[s][:], g.x[sq, t0:t0 + 256, :].rearrange("(j p) d -> p j d", p=128), bxtok[s], writes=[bxtok[s]])
        for c in range(8):
            bank = s * 4 + c // 2
            for j in range(2):
                k.tr(ps[bank][:, (c % 2) * 256 + j * 128:(c % 2) * 256 + (j + 1) * 128], xtok[s][:, j, c * 128:(c + 1) * 128],
                     g.ident[:], [bxtok[s], g.bconst], [bps[bank]])
        for q in range(4):
            bank = s * 4 + q
            k.copy("act" if q % 2 else "dve", xt[s][:, 2 * q:2 * q + 2, :].rearrange("p c t -> p (c t)"), ps[bank][:],
                   [bps[bank]], [bxt[s]])
        k.dma("sp", g.xTv[:, :, b * 256:(b + 1) * 256], xt[s][:], bxt[s], reads=[bxt[s]], writes=[g.bxT[b]])


def phase_tout(g):
    k = g.k
    xt = [k.sb([128, 8, 256], F32) for _ in range(2)]
    bxt = [Buf(), Buf()]
    ytok = [k.sb([128, 2, D], F32) for _ in range(2)]
    bytok = [Buf(), Buf()]
    ps = [k.psum([128, 512], F32) for _ in range(8)]
    bps = [Buf() for _ in range(8)]
    by = Buf("y")
    for b in range(NT // 256):
        s = b % 2
        sq, t0 = divmod(b * 256, T)
        k.dma("sp", xt[s][:], g.xTv[:, :, b * 256:(b + 1) * 256], bxt[s], reads=[g.bxT[b]], writes=[bxt[s]])
        for j in range(2):
            for c in range(8):
                bank = s * 4 + j * 2 + c // 4
                k.tr(ps[bank][:, (c % 4) * 128:(c % 4 + 1) * 128], xt[s][:, c, j * 128:(j + 1) * 128], g.ident[:],
                     [bxt[s], g.bconst], [bps[bank]])
        for q in range(4):
            bank = s * 4 + q
            k.copy("act" if q % 2 else "dve", ytok[s][:, q // 2, (q % 2) * 512:(q % 2 + 1) * 512], ps[bank][:], [bps[bank]], [bytok[s]])
        k.dma("sp", g.y[sq, t0:t0 + 256, :].rearrange("(j p) d -> p j d", p=128), ytok[s][:], bytok[s], reads=[bytok[s]], writes=[by])


def norm_mod(g, xt, bxt, hT, bhT, i, sq, n, N, st):
    k = g.k
    for c in range(8):
        s2 = c % 2
        k.act(st["sq"][s2][:, :N], xt[:, c, :N], AF.Square, [bxt], [st["bsq"][s2]])
        k.mm(st["ps"][:, :N], g.onesb[:], st["sq"][s2][:, :N], c == 0, c == 7, [st["bsq"][s2], g.bconst], [st["bps"]])
    if USE_ARS:
        k.act(st["rstd"][:, :N], st["ps"][:, :N], AF.Abs_reciprocal_sqrt, [st["bps"], g.bconst], [st["brstd"]], bias=g.epsc[:, 0:1], scale=1.0 / D)
    else:
        k.act(st["rt"][:, :N], st["ps"][:, :N], AF.Sqrt, [st["bps"], g.bconst], [st["brt"]], bias=g.epsc[:, 0:1], scale=1.0 / D)
        k.recip(st["rstd"][:, :N], st["rt"][:, :N], [st["brt"]], [st["brstd"]])
    for c in range(8):
        s2 = c % 2
        k.stt(st["tmp"][s2][:, :N], xt[:, c, :N], g.gm[:, i, sq, n, c:c + 1], st["rstd"][:, :N], ALU.mult, ALU.mult,
              [bxt, st["brstd"], g.bmod], [st["btmp"][s2]])
        k.act(hT[:, c, :N], st["tmp"][s2][:, :N], AF.Identity, [st["btmp"][s2], g.bmod], [bhT],
              bias=g.modT[:, i, sq, 3 * n * 8 + c:3 * n * 8 + c + 1])


def norm_scratch(g, N):
    k = g.k
    st = {}
    st["sq"] = [k.sb([128, N], BF16) for _ in range(2)]
    st["bsq"] = [Buf(), Buf()]
    st["ps"] = k.psum([128, 512], F32)
    st["bps"] = Buf()
    st["rt"] = k.sb([128, N], F32)
    st["brt"] = Buf()
    st["rstd"] = k.sb([128, N], F32)
    st["brstd"] = Buf()
    st["tmp"] = [k.sb([128, N], F32) for _ in range(2)]
    st["btmp"] = [Buf(), Buf()]
    return st


def phase_ffn(g, i, which):
    k = g.k
    n = 0 if which == 0 else 2
    N = 256
    w_in = k.sb([128, 8, 2 * DFF], BF16)
    w_out = k.sb([128, FC, D], BF16)
    bwo = Buf("w_out")
    bwiq = [Buf("w_in%d" % q) for q in range(2)]
    wiv = g.ffn_w_in[i, which].rearrange("(c p) f -> p c f", p=128)
    wov = g.ffn_w_out[i, which].rearrange("(j p) d -> p j d", p=128)
    HJ = 11 * 128
    for q in range(2):
        for gu in range(2):
            for ch in range(2):
                cs = slice(gu * DFF + q * HJ, gu * DFF + (q + 1) * HJ)
                k.dma("pool", w_in[:, ch * 4:(ch + 1) * 4, cs], wiv[:, ch * 4:(ch + 1) * 4, cs], bwiq[q], writes=[bwiq[q]])
    for q in range(2):
        k.dma("pool", w_out[:, q * 11:(q + 1) * 11, :], wov[:, q * 11:(q + 1) * 11, :], bwo, writes=[bwo])
    xt = [k.sb([128, 8, N], F32) for _ in range(2)]
    bxt = [Buf(), Buf()]
    hT = [k.sb([128, 8, N], BF16) for _ in range(2)]
    bhT = [Buf(), Buf()]
    actT = [k.sb([128, FC, N], BF16) for _ in range(2)]
    bact = [Buf(), Buf()]
    sg = [k.sb([128, N], F32) for _ in range(2)]
    bsg = [Buf(), Buf()]
    st = norm_scratch(g, N)
    psg = [k.psum([128, 512], F32) for _ in range(2)]
    psu = [k.psum([128, 512], F32) for _ in range(2)]
    pso = [k.psum([128, 512], F32) for _ in range(2)]
    bpsg, bpsu, bpso = [Buf(), Buf()], [Buf(), Buf()], [Buf(), Buf()]
    nb = NT // N

    def load(b):
        s = b % 2
        k.dma("sp", xt[s][:], g.xTv[:, :, b * N:(b + 1) * N], bxt[s], reads=[g.bxT[b]], writes=[bxt[s]])

    load(0)
    norm_mod(g, xt[0], bxt[0], hT[0], bhT[0], i, 0, n, N, st)
    for b in range(nb):
        s = b % 2
        sq = (b * N) // T
        if b + 1 < nb:
            load(b + 1)
        for j in range(FC):
            p2 = j % 2
            bwi = bwiq[j // 11]
            for c in range(8):
                k.mm(psg[p2][:, :N], w_in[:, c, j * 128:(j + 1) * 128], hT[s][:, c, :], c == 0, c == 7, [bwi, bhT[s]], [bpsg[p2]])
            for c in range(8):
                k.mm(psu[p2][:, :N], w_in[:, c, DFF + j * 128:DFF + (j + 1) * 128], hT[s][:, c, :], c == 0, c == 7,
                     [bwi, bhT[s]], [bpsu[p2]])
            k.act(sg[p2][:], psg[p2][:, :N], AF.Silu, [bpsg[p2]], [bsg[p2]])
            k.tt("dve", actT[s][:, j, :], sg[p2][:], psu[p2][:, :N], ALU.mult, [bsg[p2], bpsu[p2]], [bact[s]])
        for dc in range(8):
            p2 = dc % 2
            if dc == 3 and b + 1 < nb:
                norm_mod(g, xt[1 - s], bxt[1 - s], hT[1 - s], bhT[1 - s], i, ((b + 1) * N) // T, n, N, st)
            for j in range(FC):
                k.mm(pso[p2][:, :N], w_out[:, j, dc * 128:(dc + 1) * 128], actT[s][:, j, :], j == 0, j == FC - 1,
                     [bwo, bact[s]], [bpso[p2]])
            k.stt(xt[s][:, dc, :], pso[p2][:, :N], g.hg[:, i, sq, n, dc:dc + 1], xt[s][:, dc, :], ALU.mult, ALU.add,
                  [bpso[p2], g.bmod, bxt[s]], [bxt[s]])
        k.dma("sp", g.xTv[:, :, b * N:(b + 1) * N], xt[s][:], bxt[s], reads=[bxt[s]], writes=[g.bxT[b]])


def phase_mix(g, i):
    kind = i % 3
    if kind == 0:
        phase_diffattn(g, i)
    elif kind == 1:
        phase_rwkv(g, i)
    else:
        phase_swa(g, i)


def diff_lambda_init(layer):
    return 0.8 - 0.6 * math.exp(-0.3 * layer)


def qk_norm_chunk(g, ps, bps, out, gvec, st, N, reads_extra=()):
    k = g.k
    s2 = st["i"] % 2
    st["i"] += 1
    k.act(st["sq"][s2][:, :N], ps, AF.Square, [bps], [st["bsq"][s2]])
    k.mm(st["pss"][s2][:, :N], g.blk64[:], st["sq"][s2][:, :N], True, True, [st["bsq"][s2], g.bconst], [st["bpss"][s2]])
    k.act(st["rt"][s2][:, :N], st["pss"][s2][:, :N], AF.Sqrt, [st["bpss"][s2], g.bconst], [st["brt"][s2]],
          bias=g.epsc[:, 0:1], scale=1.0 / 64)
    k.recip(st["rs"][s2][:, :N], st["rt"][s2][:, :N], [st["brt"][s2]], [st["brs"][s2]])
    k.stt(out, ps, gvec, st["rs"][s2][:, :N], ALU.mult, ALU.mult, [bps, st["brs"][s2]] + list(reads_extra), st["wout"])


def qk_scratch(g, N, nps=2):
    k = g.k
    st = {"i": 0}
    st["sq"] = [k.sb([128, N], BF16) for _ in range(2)]
    st["bsq"] = [Buf(), Buf()]
    st["pss"] = [k.psum([128, 512], F32) for _ in range(nps)]
    st["bpss"] = [Buf() for _ in range(nps)]
    if nps == 0:
        pass
    elif nps == 1:
        st["pss"] = st["pss"] * 2
        st["bpss"] = st["bpss"] * 2
    st["rt"] = [k.sb([128, N], F32) for _ in range(2)]
    st["brt"] = [Buf(), Buf()]
    st["rs"] = [k.sb([128, N], F32) for _ in range(2)]
    st["brs"] = [Buf(), Buf()]
    return st


def load_vec128(g, dst, src64, bdst):
    k = g.k
    v = src64.rearrange("(p o) -> p o", o=1)
    k.dma("sp", dst[0:64, :], v, bdst, writes=[bdst])
    k.dma("sp", dst[64:128, :], v, bdst, writes=[bdst])


def phase_diffattn(g, i):
    k = g.k
    nc = g.nc
    j = i // 3
    N = 256
    lam_init = diff_lambda_init(i)
    kd = "ExternalOutput" if DBG_EXT else "Internal"
    qT_d = nc.dram_tensor("da_qT%d" % i, [NS, 8, 128, T], BF16, kind=kd).ap()
    kT_d = nc.dram_tensor("da_kT%d" % i, [NS, 8, 128, T], BF16, kind=kd).ap()
    v_d = nc.dram_tensor("da_v%d" % i, [NS, T, D], BF16, kind=kd).ap()
    oT_d = nc.dram_tensor("da_oT%d" % i, [NS, 8, 128, T], BF16, kind=kd).ap()
    nb = NT // N
    bq_d = [Buf() for _ in range(nb)]
    bo_d = [[Buf() for _ in range(8)] for _ in range(NS)]
    outer = k.pes
    with ExitStack() as pes:
        k.pes = pes
        wqk = k.sb([128, 8, 2048], BF16)
        wv = k.sb([128, 8, 1024], BF16)
        bw = Buf()
        wsrc = g.da_w_qkv[j].rearrange("(c p) n -> p c n", p=128)
        for c in range(8):
            k.dma("pool", wqk[:, c, :], wsrc[:, c, 0:2048], bw, writes=[bw])
        for q in range(2):
            k.dma("pool", wv[:, q * 4:(q + 1) * 4, :], wsrc[:, q * 4:(q + 1) * 4, 2048:3072], bw, writes=[bw])
        gq = k.sb([128, 2], F32)
        bgq = Buf()
        load_vec128(g, gq[:, 0:1], g.da_q_norm_g[j], bgq)
        load_vec128(g, gq[:, 1:2], g.da_k_norm_g[j], bgq)
        k.ts("dve", gq[:, 0:1], gq[:, 0:1], 0.125, None, ALU.mult, None, [bgq], [bgq])
        xt = [k.sb([128, 8, N], F32) for _ in range(2)]
        bxt = [Buf(), Buf()]
        hT = [k.sb([128, 8, N], BF16) for _ in range(2)]
        bhT = [Buf(), Buf()]
        qblk = [k.sb([128, 8, N], BF16) for _ in range(2)]
        kblk = [k.sb([128, 8, N], BF16) for _ in range(2)]
        vblk = [k.sb([128, 2, D], BF16) for _ in range(2)]
        bqb, bkb, bvb = [Buf(), Buf()], [Buf(), Buf()], [Buf(), Buf()]
        st = norm_scratch(g, N)
        qs = qk_scratch(g, N)
        psq = [k.psum([128, 512], F32) for _ in range(3)]
        bpsq = [Buf(), Buf(), Buf()]
        psv = [k.psum([128, 512], F32) for _ in range(2)]
        bpsv = [Buf(), Buf()]

        def load(b):
            s = b % 2
            k.dma("sp", xt[s][:], g.xTv[:, :, b * N:(b + 1) * N], bxt[s], reads=[g.bxT[b]], writes=[bxt[s]])

        load(0)
        norm_mod(g, xt[0], bxt[0], hT[0], bhT[0], i, 0, 1, N, st)
        for b in range(nb):
            s = b % 2
            sq, t0 = divmod(b * N, T)
            if b + 1 < nb:
                load(b + 1)
            jobs = []
            for isk in range(2):
                dst, bdst = (qblk[s], bqb[s]) if isk == 0 else (kblk[s], bkb[s])
                for h in range(8):
                    jobs.append((isk * 1024 + h * 128, dst[:, h, :], gq[:, isk:isk + 1], bdst))
            nj = len(jobs)
            for ii in range(nj + 2):
                if ii < nj:
                    col = jobs[ii][0]
                    p3 = ii % 3
                    for c in range(8):
                        k.mm(psq[p3][:, :N], wqk[:, c, col:col + 128], hT[s][:, c, :], c == 0, c == 7, [bw, bhT[s]], [bpsq[p3]])
                if 1 <= ii <= nj:
                    q_ = ii - 1
                    p3, s2 = q_ % 3, q_ % 2
                    k.act(qs["sq"][s2][:, :N], psq[p3][:, :N], AF.Square, [bpsq[p3]], [qs["bsq"][s2]])
                    k.mm(qs["pss"][s2][:, :N], g.blk64[:], qs["sq"][s2][:, :N], True, True, [qs["bsq"][s2], g.bconst], [qs["bpss"][s2]])
                if ii >= 2:
                    q_ = ii - 2
                    p3, s2 = q_ % 3, q_ % 2
                    _, outap, gvec, bdst = jobs[q_]
                    k.act(qs["rt"][s2][:, :N], qs["pss"][s2][:, :N], AF.Sqrt, [qs["bpss"][s2], g.bconst], [qs["brt"][s2]],
                          bias=g.epsc[:, 0:1], scale=1.0 / 64)
                    k.recip(qs["rs"][s2][:, :N], qs["rt"][s2][:, :N], [qs["brt"][s2]], [qs["brs"][s2]])
                    k.stt(outap, psq[p3][:, :N], gvec, qs["rs"][s2][:, :N], ALU.mult, ALU.mult, [bpsq[p3], qs["brs"][s2], bgq], [bdst])
            if b + 1 < nb:
                norm_mod(g, xt[1 - s], bxt[1 - s], hT[1 - s], bhT[1 - s], i, ((b + 1) * N) // T, 1, N, st)
            for jt in range(2):
                for half in range(2):
                    p2 = (jt * 2 + half) % 2
                    for c in range(8):
                        k.mm(psv[p2][:], hT[s][:, c, jt * 128:(jt + 1) * 128], wv[:, c, half * 512:(half + 1) * 512], c == 0, c == 7,
                             [bw, bhT[s]], [bpsv[p2]])
                    k.copy("act" if half else "dve", vblk[s][:, jt, half * 512:(half + 1) * 512], psv[p2][:], [bpsv[p2]], [bvb[s]])
            k.dma("sp", qT_d[sq, :, :, t0:t0 + N].rearrange("h p t -> p h t"), qblk[s][:], bqb[s], reads=[bqb[s]], writes=[bq_d[b]])
            k.dma("sp", kT_d[sq, :, :, t0:t0 + N].rearrange("h p t -> p h t"), kblk[s][:], bkb[s], reads=[bkb[s]], writes=[bq_d[b]])
            k.dma("sp", v_d[sq, t0:t0 + N, :].rearrange("(j p) e -> p j e", p=128), vblk[s][:], bvb[s], reads=[bvb[s]], writes=[bq_d[b]])
        k.barrier()
    with ExitStack() as pes:
        k.pes = pes
        QB = 256
        lamt = k.sb([128, 256], F32)
        blam = Buf()
        k.dma("sp", lamt[:], g.da_lambda[j:j + 1].rearrange("o a b -> o (a b)").broadcast_to([128, 256]), blam, writes=[blam])
        lprod = k.sb([128, 2, 64], F32)
        k.tt("dve", lprod[:, 0, :], lamt[:, 0:64], lamt[:, 64:128], ALU.mult, [blam], [blam])
        k.tt("dve", lprod[:, 1, :], lamt[:, 128:192], lamt[:, 192:256], ALU.mult, [blam], [blam])
        lsum = k.sb([128, 2], F32)
        k.op("dve", lambda e: e.tensor_reduce(lsum[:], lprod[:], AX.X, ALU.add), [blam], [blam])
        lexp = k.sb([128, 2], F32)
        k.act(lexp[:], lsum[:], AF.Exp, [blam], [blam])
        negl = k.sb([128, 1], F32)
        k.tt("dve", negl[:], lexp[:, 1:2], lexp[:, 0:1], ALU.subtract, [blam], [blam])
        k.ts("dve", negl[:], negl[:], -lam_init, None, ALU.add, None, [blam], [blam])
        gsub = k.sb([128, 128], F32)
        k.dma("sp", gsub[:], g.da_subln_g[j:j + 1].broadcast_to([128, 128]), blam, writes=[blam])
        k.ts("dve", gsub[:], gsub[:], 1.0 - lam_init, None, ALU.mult, None, [blam], [blam])
        tri = k.sb([128, 128], BF16)
        io = k.sb([128, 128], I32)
        k.op("pool", lambda e: e.iota(io[:], [[1, 128]], base=0, channel_multiplier=-1), (), [blam])
        k.ts("dve", tri[:], io[:], 0, None, ALU.is_ge, None, [blam], [blam])
        kT = [k.sb([128, T], BF16) for _ in range(2)]
        qT = [k.sb([128, 2, T], BF16) for _ in range(2)]
        va = [k.sb([128, 32, 129], BF16) for _ in range(2)]
        bkv = [Buf(), Buf()]
        for s in range(2):
            k.memset("dve", va[s][:, :, 128:129], 1.0, [bkv[s]])
            k.memset("dve", qT[s][:], 0.0, [bkv[s]])
        NP = 8
        LAG = 4
        pT = [k.sb([128, 2, QB], BF16) for _ in range(NP)]
        bpT = [Buf() for _ in range(NP)]
        NSB = 5
        pss = [k.psum([128, 512], F32) for _ in range(NSB)]
        bpss = [Buf() for _ in range(NSB)]
        accb = [k.psum([128, 512], F32) for _ in range(2)]
        acc = [[accb[c][:, sb * 256:sb * 256 + 129] for sb in range(2)] for c in range(2)]
        bacc_ = [Buf(), Buf()]
        bacc = [[bacc_[0], bacc_[0]], [bacc_[1], bacc_[1]]]
        NF = 3
        osb = [k.sb([128, 2, 2, 129], F32) for _ in range(NF)]
        rr = [k.sb([128, 2, 2], F32) for _ in range(NF)]
        tt_ = [k.sb([128, 2, 2, 128], F32) for _ in range(NF)]
        ds = [k.sb([128, 2, 128], F32) for _ in range(NF)]
        sqt = [k.sb([128, 2, 128], F32) for _ in range(NF)]
        ssv = [k.sb([128, 2, 2], F32) for _ in range(NF)]
        on = [k.sb([128, 2, 128], BF16) for _ in range(NF)]
        bfin = [Buf() for _ in range(NF)]
        mhalf = k.sb([128, 2], F32)
        k.memset("dve", mhalf[:], -0.5, [blam])
        pst = k.psum([128, 2, 128], BF16)
        bpst = Buf()
        oblk = [k.sb([128, QB], BF16) for _ in range(2)]
        bob = [Buf(), Buf()]
        st8 = {"ip": 0, "ifin": 0, "iob": 0}
        dq = []

        def tick():
            for it in dq:
                it[0] -= 1
            while dq and dq[0][0] <= 0:
                dq.pop(0)[1]()

        def fin_a(sq, h, q0):
            f = st8["ifin"] % NF
            st8["ifin"] += 1
            for c in range(2):
                for sb in range(2):
                    k.copy("dve", osb[f][:, c, sb, :], acc[c][sb], [bacc[c][sb]], [bfin[f]])
            k.recip(rr[f][:], osb[f][:, :, :, 128], [bfin[f]], [bfin[f]])
            k.ts("dve", rr[f][:, 1, :], rr[f][:, 1, :], negl[:, 0:1], None, ALU.mult, None, [bfin[f], blam], [bfin[f]])
            k.tt("dve", tt_[f][:], osb[f][:, :, :, 0:128], rr[f][:].unsqueeze(3).broadcast_to([128, 2, 2, 128]), ALU.mult, [bfin[f]], [bfin[f]])
            k.tt("dve", ds[f][:], tt_[f][:, 0], tt_[f][:, 1], ALU.add, [bfin[f]], [bfin[f]])
            k.tt("dve", sqt[f][:], ds[f][:], ds[f][:], ALU.mult, [bfin[f]], [bfin[f]])
            k.op("dve", lambda e: e.tensor_reduce(ssv[f][:, 0, :], sqt[f][:], AX.X, ALU.add), [bfin[f]], [bfin[f]])
            k.ts("dve", ssv[f][:, 0, :], ssv[f][:, 0, :], 1.0 / 128, 1e-5, ALU.mult, ALU.add, [bfin[f]], [bfin[f]])
            dq.append([6, lambda: fin_b(f, sq, h, q0)])

        def fin_b(f, sq, h, q0):
            k.act(ssv[f][:, 1, :], ssv[f][:, 0, :], AF.Sqrt, [bfin[f]], [bfin[f]])
            dq.append([6, lambda: fin_c(f, sq, h, q0)])

        def fin_c(f, sq, h, q0):
            k.recip(ssv[f][:, 1, :], ssv[f][:, 1, :], [bfin[f]], [bfin[f]])
            k.tt("dve", ds[f][:], ds[f][:], ssv[f][:, 1, :].unsqueeze(2).broadcast_to([128, 2, 128]), ALU.mult, [bfin[f]], [bfin[f]])
            k.tt("dve", on[f][:], ds[f][:], gsub[:, None, :].broadcast_to([128, 2, 128]), ALU.mult, [bfin[f], blam], [bfin[f]])
            ob = st8["iob"] % 2
            st8["iob"] += 1
            for sb in range(2):
                k.tr(pst[:, sb, :], on[f][:, sb, :], g.identb[:], [bfin[f], g.bconst], [bpst])
            k.copy("dve", oblk[ob][:], pst[:].rearrange("p a q -> p (a q)"), [bpst], [bob[ob]])
            k.dma("sp", oT_d[sq, h, :, q0:q0 + QB], oblk[ob][:], bob[ob], reads=[bob[ob]], writes=[bo_d[sq][h]])

        pend = []

        def do_pv(it):
            kt, pp, off, n0, hs, qb, last, sq, h = it
            for c in range(2):
                for sb in range(2):
                    if sb * 128 < n0:
                        continue
                    k.mm(acc[c][sb], pT[pp][:, c, sb * 128:(sb + 1) * 128], va[hs][:, kt, :], kt == 0 and sb == 0, kt == 2 * qb + sb,
                         [bpT[pp], bkv[hs]], [bacc[c][sb]], skip_group_check=True)
            if last:
                fin_a(sq, h, qb * QB)

        for sq in range(NS):
            for h in range(8):
                hs = (sq * 8 + h) % 2
                deps = bq_d[sq * (T // N):(sq + 1) * (T // N)]
                k.dma("sp", kT[hs][:], kT_d[sq, h], bkv[hs], reads=deps, writes=[bkv[hs]])
                for c in range(2):
                    k.dma("sp", qT[hs][c * 64:(c + 1) * 64, c, :], qT_d[sq, h, c * 64:(c + 1) * 64, :], bkv[hs], reads=deps, writes=[bkv[hs]])
                k.dma("sp", va[hs][:, :, 0:128], v_d[sq, :, h * 128:(h + 1) * 128].rearrange("(n p) e -> p n e", p=128), bkv[hs],
                      reads=deps, writes=[bkv[hs]])
                for qb in range(T // QB):
                    q0 = qb * QB
                    nkt = 2 * qb + 2
                    for kt in range(nkt):
                        off = kt * 128 - q0
                        n0 = max(off, 0)
                        ip = st8["ip"]
                        st8["ip"] += 1
                        sp_ = ip % NSB
                        pp = ip % NP
                        psv_ = pss[sp_][:].rearrange("p (c q) -> p c q", c=2)
                        for c in range(2):
                            k.mm(psv_[:, c, n0:QB], kT[hs][:, kt * 128:(kt + 1) * 128], qT[hs][:, c, q0 + n0:q0 + QB], True, True,
                                 [bkv[hs]], [bpss[sp_]])
                        k.act(pT[pp][:, :, n0:QB], psv_[:, :, n0:QB], AF.Exp, [bpss[sp_]], [bpT[pp]])
                        if off >= 0:
                            k.tt("pool", pT[pp][:, :, off:off + 128], pT[pp][:, :, off:off + 128], tri[:, None, :].broadcast_to([128, 2, 128]),
                                 ALU.mult, [bpT[pp], blam], [bpT[pp]])
                        pend.append((kt, pp, off, n0, hs, qb, kt == nkt - 1, sq, h))
                        if len(pend) > LAG:
                            do_pv(pend.pop(0))
                        tick()
                        tick()
        while pend:
            do_pv(pend.pop(0))
        while dq:
            dq.pop(0)[1]()
        k.barrier()
    with ExitStack() as pes:
        k.pes = pes
        wo = k.sb([128, 8, D], BF16)
        bwo = Buf()
        k.dma("pool", wo[:], g.da_w_o[j].rearrange("(c p) n -> p c n", p=128), bwo, writes=[bwo])
        out_proj(g, i, wo, bwo, oT_d, bo_d)
        k.barrier()
    k.pes = outer


def out_proj(g, i, wo, bwo, oT_d, bo_d):
    k = g.k
    N = 256
    nb = NT // N
    xt = [k.sb([128, 8, N], F32) for _ in range(2)]
    bxt = [Buf(), Buf()]
    oT = [k.sb([128, 8, N], BF16) for _ in range(2)]
    boT = [Buf(), Buf()]
    pso = [k.psum([128, 512], F32) for _ in range(2)]
    bpso = [Buf(), Buf()]

    def load(b):
        s = b % 2
        sq, t0 = divmod(b * N, T)
        k.dma("sp", xt[s][:], g.xTv[:, :, b * N:(b + 1) * N], bxt[s], reads=[g.bxT[b]], writes=[bxt[s]])
        k.dma("sp", oT[s][:], oT_d[sq, :, :, t0:t0 + N].rearrange("h p t -> p h t"), boT[s], reads=bo_d[sq], writes=[boT[s]])

    load(0)
    for b in range(nb):
        s = b % 2
        sq = (b * N) // T
        if b + 1 < nb:
            load(b + 1)
        for dc in range(8):
            p2 = dc % 2
            for hh in range(8):
                k.mm(pso[p2][:, :N], wo[:, hh, dc * 128:(dc + 1) * 128], oT[s][:, hh, :], hh == 0, hh == 7, [bwo, boT[s]], [bpso[p2]])
            k.stt(xt[s][:, dc, :], pso[p2][:, :N], g.hg[:, i, sq, 1, dc:dc + 1], xt[s][:, dc, :], ALU.mult, ALU.add,
                  [bpso[p2], g.bmod, bxt[s]], [bxt[s]])
        k.dma("sp", g.xTv[:, :, b * N:(b + 1) * N], xt[s][:], bxt[s], reads=[bxt[s]], writes=[g.bxT[b]])


C0 = -math.exp(-0.5)


class TB:
    __slots__ = ("t", "b")

    def __init__(self, t):
        self.t = t
        self.b = Buf()


def phase_rwkv(g, i):
    k = g.k
    nc = g.nc
    j = i // 3
    NCH = NT // 128
    stage_d = nc.dram_tensor("rw_stage", [NCH, 128, 7, D], BF16, kind="Internal").ap()
    g_d = nc.dram_tensor("rw_gate", [NCH, 128, D], F32, kind="Internal").ap()
    outer = k.pes

    def sbt(shape, dt):
        return TB(k.sb(shape, dt))

    with ExitStack() as ph:
        k.pes = ph
        bc_all = sbt([128, NCH, 16], F32)
        gC_all = sbt([128, NCH, 8], F32)
        cst = sbt([128, 6, 128], F32)
        io = k.sb([128, 128], I32)
        k.op("pool", lambda e: e.iota(io[:], [[1, 128]], base=0, channel_multiplier=-1), (), [cst.b])
        k.ts("dve", cst.t[:, 0, :], io[:], 0, None, ALU.is_gt, None, [cst.b], [cst.b])
        k.ts("dve", cst.t[:, 1, :], io[:], 0, None, ALU.is_ge, None, [cst.b], [cst.b])
        k.ts("dve", cst.t[:, 2, :], io[:], 0, None, ALU.is_lt, None, [cst.b], [cst.b])
        k.ts("dve", cst.t[:, 3, :], cst.t[:, 1, :], C0, None, ALU.mult, None, [cst.b], [cst.b])
        k.memset("dve", cst.t[:, 4, :], C0, [cst.b])
        k.copy("dve", cst.t[:, 5, :], g.ident[:], [cst.b, g.bconst], [cst.b])
        banks = [TB(k.psum([128, 512], F32)) for _ in range(7)]
        pstb = TB(k.psum([128, 1024], BF16))
        bi = [0]

        def bank():
            b = banks[bi[0] % 7]
            bi[0] += 1
            return b

        with ExitStack() as pes:
            k.pes = pes
            wrkv = sbt([128, 3, 8, D], BF16)
            for n in range(3):
                for q in range(2):
                    k.dma("pool", wrkv.t[:, n, q * 4:(q + 1) * 4, :], g.rw_w_rkv[j, n].rearrange("(c p) n -> p c n", p=128)[:, q * 4:(q + 1) * 4, :],
                          wrkv.b, writes=[wrkv.b])
            wl1 = sbt([128, 2, 8, 64], BF16)
            k.dma("pool", wl1.t[:, 0], g.rw_decay_w1[j].rearrange("(c p) n -> p c n", p=128), wl1.b, writes=[wl1.b])
            k.dma("pool", wl1.t[:, 1], g.rw_iclr_a1[j].rearrange("(c p) n -> p c n", p=128), wl1.b, writes=[wl1.b])
            wg1 = sbt([128, 8, 160], BF16)
            k.dma("pool", wg1.t[:], g.rw_gate_g1[j].rearrange("(c p) n -> p c n", p=128), wg1.b, writes=[wg1.b])
            wl2 = sbt([64, 2, D], BF16)
            k.dma("pool", wl2.t[:, 0, :], g.rw_decay_w2[j], wl2.b, writes=[wl2.b])
            k.dma("pool", wl2.t[:, 1, :], g.rw_iclr_a2[j], wl2.b, writes=[wl2.b])
            wg2 = sbt([128, 2, D], BF16)
            k.dma("pool", wg2.t[:, 0, :], g.rw_gate_g2[j, 0:128, :], wg2.b, writes=[wg2.b])
            k.memset("dve", wg2.t[:, 1, :], 0.0, [wg2.b])
            k.dma("pool", wg2.t[0:32, 1, :], g.rw_gate_g2[j, 128:160, :], wg2.b, writes=[wg2.b])
            bct = sbt([128, 5, D], F32)
            for q, src in enumerate((g.rw_decay_w0[j:j + 1], g.rw_iclr_a0[j:j + 1], g.rw_k_k[j:j + 1], g.rw_k_a[j:j + 1],
                                     g.rw_r_k[j:j + 1].rearrange("o h d -> o (h d)"))):
                k.dma("sp", bct.t[:, q, :], src.broadcast_to([128, D]), bct.b, writes=[bct.b])
            muT = sbt([128, 6, 8], F32)
            murow = sbt([48, 128], F32)
            k.dma("sp", murow.t[:], g.rw_mu[j].rearrange("n (c p) -> (n c) p", p=128), murow.b, writes=[murow.b])
            pb = bank()
            k.tr(pb.t[:, 0:48], murow.t[:], g.ident[:48, :48], [murow.b, g.bconst], [pb.b])
            k.copy("dve", muT.t[:].rearrange("p n c -> p (n c)"), pb.t[:, 0:48], [pb.b], [muT.b])
            c0col = sbt([128, 2], F32)
            k.memset("dve", c0col.t[:], C0, [c0col.b])
            xt = [sbt([128, 8, 128], F32) for _ in range(2)]
            hT = [sbt([128, 8, 129], F32) for _ in range(2)]
            hlast = sbt([128, 8, 1], F32)
            xx = sbt([128, 8, 128], F32)
            mixed = [sbt([128, 8, 128], BF16) for _ in range(2)]
            lor = [sbt([128, 128], BF16) for _ in range(2)]
            lor2 = sbt([128, 128], BF16)
            k.memset("dve", lor2.t[:], 0.0, [lor2.b])
            r32_, k32_, a32_, lw_ = [[sbt([128, D], F32) for _ in range(2)] for _ in range(4)]
            cl32, kk32, E1, E2 = [sbt([128, D], F32) for _ in range(4)]
            q1 = []

            def drain1(nops):
                for _ in range(nops):
                    if q1:
                        q1.pop(0)()
            g32 = [sbt([128, D], F32) for _ in range(2)]
            stage = [sbt([128, 7, D], BF16) for _ in range(2)]
            small = sbt([128, 3, 16], F32)
            sqb = [sbt([128, 128], BF16) for _ in range(2)]
            rstd = sbt([128, 2, 128], F32)
            tmpn = [sbt([128, 128], F32) for _ in range(2)]

            def v3(ap):
                return ap.rearrange("p (h d) -> p h d", d=64)

            def load1(ci):
                s = ci % 2
                k.dma("sp", xt[s].t[:], g.xTv[:, :, ci * 128:(ci + 1) * 128], xt[s].b, reads=[g.bxT[ci // 2]], writes=[xt[s].b])

            load1(0)
            for ci in range(NCH):
                s = ci % 2
                sq, n = divmod(ci, NCH // NS)
                if ci + 1 < NCH:
                    load1(ci + 1)
                X, H, SG = xt[s], hT[s], stage[s]
                r32, k32, a32, lw = r32_[s], k32_[s], a32_[s], lw_[s]
                pn = bank()
                for c in range(8):
                    k.act(sqb[c % 2].t[:], X.t[:, c, :], AF.Square, [X.b], [sqb[c % 2].b])
                    k.mm(pn.t[:, :128], g.onesb[:], sqb[c % 2].t[:], c == 0, c == 7, [sqb[c % 2].b, g.bconst], [pn.b])
                k.act(rstd.t[:, 0, :], pn.t[:, :128], AF.Sqrt, [pn.b, g.bconst], [rstd.b], bias=g.epsc[:, 0:1], scale=1.0 / D)
                k.recip(rstd.t[:, 1, :], rstd.t[:, 0, :], [rstd.b], [rstd.b])
                if n == 0:
                    k.memset("dve", H.t[:, :, 0:1], 0.0, [H.b])
                else:
                    k.copy("dve", H.t[:, :, 0:1], hlast.t[:], [hlast.b], [H.b])
                for c in range(8):
                    k.stt(tmpn[c % 2].t[:], X.t[:, c, :], g.gm[:, i, sq, 1, c:c + 1], rstd.t[:, 1, :], ALU.mult, ALU.mult,
                          [X.b, rstd.b, g.bmod], [tmpn[c % 2].b])
                    k.act(H.t[:, c, 1:129], tmpn[c % 2].t[:], AF.Identity, [tmpn[c % 2].b, g.bmod], [H.b],
                          bias=g.modT[:, i, sq, 24 + c:24 + c + 1])
                k.copy("dve", hlast.t[:], H.t[:, :, 128:129], [H.b], [hlast.b])
                k.tt("dve", xx.t[:], H.t[:, :, 0:128], H.t[:, :, 1:129], ALU.subtract, [H.b], [xx.b])
                for n6 in range(6):
                    M = mixed[n6 % 2]
                    for c in range(8):
                        k.stt(M.t[:, c, :], xx.t[:, c, :], muT.t[:, n6, c:c + 1], H.t[:, c, 1:129], ALU.mult, ALU.add,
                              [xx.b, muT.b, H.b], [M.b])
                    if n6 < 3:
                        for half in range(2):
                            pb = bank()
                            for c in range(8):
                                k.mm(pb.t[:], M.t[:, c, :], wrkv.t[:, n6, c, half * 512:(half + 1) * 512], c == 0, c == 7,
                                     [M.b, wrkv.b], [pb.b])
                            hs = slice(half * 512, (half + 1) * 512)
                            if n6 == 0:
                                k.copy("act", r32.t[:, hs], pb.t[:], [pb.b], [r32.b])
                            elif n6 == 1:
                                k.copy("act", k32.t[:, hs], pb.t[:], [pb.b], [k32.b])
                            else:
                                k.copy("act", SG.t[:, 6, hs], pb.t[:], [pb.b], [SG.b])
                    elif n6 < 5:
                        li = n6 - 3
                        L = lor[li]
                        pb = bank()
                        for c in range(8):
                            k.mm(pb.t[0:64, 0:128], wl1.t[:, li, c, :], M.t[:, c, :], c == 0, c == 7, [M.b, wl1.b], [pb.b])
                        k.act(L.t[0:64, :], pb.t[0:64, 0:128], AF.Tanh if li == 0 else AF.Identity, [pb.b], [L.b])
                        dst = lw if li == 0 else a32
                        for half in range(2):
                            hs = slice(half * 512, (half + 1) * 512)
                            pb2 = bank()
                            k.mm(pb2.t[:], L.t[0:64, :], wl2.t[0:64, li, hs], True, True, [L.b, wl2.b], [pb2.b])
                            k.tt("dve", dst.t[:, hs], pb2.t[:], bct.t[:, li, hs], ALU.add, [pb2.b, bct.b], [dst.b])
                        k.act(dst.t[:], dst.t[:], AF.Sigmoid, [dst.b], [dst.b])
                    else:
                        L = lor[0]
                        pb = bank()
                        for c in range(8):
                            k.mm(pb.t[:, 0:128], wg1.t[:, c, 0:128], M.t[:, c, :], c == 0, c == 7, [M.b, wg1.b], [pb.b])
                        k.act(L.t[:], pb.t[:, 0:128], AF.Sigmoid, [pb.b], [L.b])
                        pb = bank()
                        for c in range(8):
                            k.mm(pb.t[0:32, 0:128], wg1.t[:, c, 128:160], M.t[:, c, :], c == 0, c == 7, [M.b, wg1.b], [pb.b])
                        k.act(lor2.t[0:32, :], pb.t[0:32, 0:128], AF.Sigmoid, [pb.b], [lor2.b])
                        G = g32[s]
                        for half in range(2):
                            hs = slice(half * 512, (half + 1) * 512)
                            pb2 = bank()
                            k.mm(pb2.t[:], L.t[:], wg2.t[:, 0, hs], True, False, [L.b, wg2.b], [pb2.b])
                            k.mm(pb2.t[:], lor2.t[:], wg2.t[:, 1, hs], False, True, [lor2.b, wg2.b], [pb2.b])
                            k.copy("act", G.t[:, hs], pb2.t[:], [pb2.b], [G.b])
                        k.dma("sp", g_d[ci], G.t[:], G.b, reads=[G.b])
                def mk_q(ci=ci, r32=r32, k32=k32, a32=a32, lw=lw, SG=SG):
                    ops = []
                    A = ops.append
                    A(lambda: k.tt("dve", kk32.t[:], k32.t[:], bct.t[:, 2, :], ALU.mult, [k32.b, bct.b], [kk32.b]))
                    A(lambda: k.tt("pool", E1.t[:], kk32.t[:], kk32.t[:], ALU.mult, [kk32.b], [E1.b]))
                    A(lambda: k.op("dve", lambda e: e.tensor_reduce(small.t[:, 0, :], v3(E1.t[:]), AX.X, ALU.add), [E1.b], [small.b]))
                    A(lambda: k.act(small.t[:, 0, :], small.t[:, 0, :], AF.Sqrt, [small.b], [small.b]))
                    A(lambda: k.stt(E1.t[:], a32.t[:], -1.0, bct.t[:, 3, :], ALU.add, ALU.mult, [a32.b, bct.b], [E1.b]))
                    A(lambda: k.stt(k32.t[:], E1.t[:], 1.0, k32.t[:], ALU.add, ALU.mult, [E1.b, k32.b], [k32.b]))
                    A(lambda: k.ts("dve", small.t[:, 0, :], small.t[:, 0, :], 1e-12, None, ALU.max, None, [small.b], [small.b]))
                    A(lambda: k.recip(small.t[:, 0, :], small.t[:, 0, :], [small.b], [small.b]))
                    A(lambda: k.tt("dve", v3(kk32.t[:]), v3(kk32.t[:]), small.t[:, 0, :].unsqueeze(2).broadcast_to([128, 16, 64]), ALU.mult,
                                   [small.b, kk32.b], [kk32.b]))
                    A(lambda: k.tt("pool", a32.t[:], a32.t[:], kk32.t[:], ALU.mult, [a32.b, kk32.b], [a32.b]))
                    A(lambda: k.tt("dve", E1.t[:], r32.t[:], k32.t[:], ALU.mult, [r32.b, k32.b], [E1.b]))
                    A(lambda: k.tt("pool", E1.t[:], E1.t[:], bct.t[:, 4, :], ALU.mult, [E1.b, bct.b], [E1.b]))
                    A(lambda: k.op("dve", lambda e: e.tensor_reduce(bc_all.t[:, ci, :], v3(E1.t[:]), AX.X, ALU.add), [E1.b], [bc_all.b]))

                    def cum(half):
                        hs = slice(half * 512, (half + 1) * 512)
                        pa = bank()
                        k.mm(pa.t[:], cst.t[:, 3, :], lw.t[:, hs], True, True, [cst.b, lw.b], [pa.b])
                        k.copy("act", cl32.t[:, hs], pa.t[:], [pa.b], [cl32.b])
                        pb = bank()
                        k.mm(pb.t[:], cst.t[:, 4, :], lw.t[:, hs], True, True, [cst.b, lw.b], [pb.b])
                        k.tt("dve", E2.t[:, hs], pb.t[:], cl32.t[:, hs], ALU.subtract, [pb.b, cl32.b], [E2.b])
                    A(lambda: cum(0))
                    A(lambda: cum(1))

                    def gc():
                        pb = bank()
                        for hp in range(8):
                            k.mm(pb.t[:, 2 * hp:2 * hp + 2], lw.t[:, hp * 128:(hp + 1) * 128], c0col.t[:], True, True, [lw.b, c0col.b], [pb.b])
                        k.act(gC_all.t[:, ci, :], pb.t[:, 0:16:2], AF.Exp, [pb.b], [gC_all.b])
                    A(gc)
                    A(lambda: k.act(E1.t[:], cl32.t[:], AF.Exp, [cl32.b], [E1.b]))
                    A(lambda: k.tt("dve", SG.t[:, 0, :], r32.t[:], E1.t[:], ALU.mult, [r32.b, E1.b], [SG.b]))
                    A(lambda: k.act(E1.t[:], cl32.t[:], AF.Exp, [cl32.b], [E1.b], scale=-1.0))
                    A(lambda: k.tt("dve", SG.t[:, 2, :], a32.t[:], E1.t[:], ALU.mult, [a32.b, E1.b], [SG.b]))
                    A(lambda: k.tt("pool", SG.t[:, 3, :], k32.t[:], E1.t[:], ALU.mult, [k32.b, E1.b], [SG.b]))
                    A(lambda: k.act(E1.t[:], E2.t[:], AF.Exp, [E2.b], [E1.b]))
                    A(lambda: k.tt("dve", SG.t[:, 4, :], a32.t[:], E1.t[:], ALU.mult, [a32.b, E1.b], [SG.b]))
                    A(lambda: k.tt("pool", SG.t[:, 5, :], k32.t[:], E1.t[:], ALU.mult, [k32.b, E1.b], [SG.b]))
                    A(lambda: k.stt(E2.t[:], lw.t[:], -C0, cl32.t[:], ALU.mult, ALU.add, [lw.b, cl32.b, E2.b], [E2.b]))
                    A(lambda: k.act(E1.t[:], E2.t[:], AF.Exp, [E2.b], [E1.b]))
                    A(lambda: k.stt(SG.t[:, 1, :], kk32.t[:], -1.0, E1.t[:], ALU.mult, ALU.mult, [kk32.b, E1.b], [SG.b]))
                    A(lambda: k.dma("sp", stage_d[ci], SG.t[:], SG.b, reads=[SG.b]))
                    return ops

                q1.extend(mk_q())
                while q1:
                    q1.pop(0)()
            k.barrier()
        with ExitStack() as pes:
            if RW_STOP == 1:
                k.pes = outer
                return
            k.pes = pes
            wo = sbt([128, 8, D], BF16)
            k.dma("pool", wo.t[:], g.rw_w_o[j].rearrange("(c p) n -> p c n", p=128), wo.b, writes=[wo.b])
            lnb = sbt([128, 2, D], F32)
            k.dma("sp", lnb.t[:, 0, :], g.rw_lnx_g[j:j + 1].broadcast_to([128, D]), lnb.b, writes=[lnb.b])
            k.dma("sp", lnb.t[:, 1, :], g.rw_lnx_b[j:j + 1].broadcast_to([128, D]), lnb.b, writes=[lnb.b])
            ST = [sbt([128, 8, 128], F32) for _ in range(NS)]
            STb = [sbt([128, 8, 128], BF16) for _ in range(NS)]
            for sq in range(NS):
                k.memset("dve", ST[sq].t[:], 0.0, [ST[sq].b])
                k.memset("dve", STb[sq].t[:], 0.0, [STb[sq].b])
            stage = [sbt([128, 7, D], BF16) for _ in range(2)]
            g32 = [sbt([128, D], F32) for _ in range(2)]
            xt = [sbt([128, 8, 128], F32) for _ in range(2)]
            aT, rT, bT, kT = [sbt([128, 8, 128], BF16) for _ in range(4)]
            arP = sbt([128, 8, 2, 2, 128], BF16)
            bP = sbt([128, 8, 2, 128], BF16)
            k.memset("dve", arP.t[:], 0.0, [arP.b])
            k.memset("dve", bP.t[:], 0.0, [bP.b])
            Y = sbt([128, 16, 128], F32)
            Pg = [[sbt([128, 4, 128], F32) for _ in range(2)] for _ in range(4)]
            PTg = [[sbt([128, 4, 128], F32) for _ in range(2)] for _ in range(4)]
            MNk = sbt([128, 8, 2, 2, 128], BF16)
            Nbr = sbt([128, 16, 128], BF16)
            WT = sbt([128, D], F32)
            UTb = sbt([128, D], BF16)
            o32, E1, E2 = [sbt([128, D], F32) for _ in range(3)]
            og = sbt([128, D], BF16)
            ogT = sbt([128, 8, 128], BF16)
            small = sbt([128, 4, 16], F32)
            ident4 = cst.t[:, 5:6, :].broadcast_to([128, 4, 128])

            def v3(ap):
                return ap.rearrange("p (h d) -> p h d", d=64)

            def load2a(idx):
                s = idx % 2
                n, sq = divmod(idx, NS)
                ci = sq * (NCH // NS) + n
                k.dma("sp", stage[s].t[:], stage_d[ci], stage[s].b, writes=[stage[s].b])

            def load2b(idx):
                s = idx % 2
                n, sq = divmod(idx, NS)
                ci = sq * (NCH // NS) + n
                k.dma("sp", g32[s].t[:], g_d[ci], g32[s].b, writes=[g32[s].b])
                k.dma("sp", xt[s].t[:], g.xTv[:, :, ci * 128:(ci + 1) * 128], xt[s].b, reads=[g.bxT[ci // 2]], writes=[xt[s].b])

            load2a(0)
            load2b(0)
            cq = []
            pe_prev = [None]
            for idx in range(NCH):
                s = idx % 2
                n, sq = divmod(idx, NS)
                ci = sq * (NCH // NS) + n
                SG, G, X = stage[s], g32[s], xt[s]
                S_, Sb = ST[sq], STb[sq]
                for (src, dstT) in ((1, aT), (0, rT), (2, bT), (3, kT)):
                    pb = pstb
                    pv = pb.t[:]
                    for hp in range(8):
                        k.tr(pv[:, hp * 128:(hp + 1) * 128], SG.t[:, src, hp * 128:(hp + 1) * 128], g.identb[:], [SG.b, g.bconst], [pb.b])
                    pv3 = pv.rearrange("p (c t) -> p c t", t=128)
                    k.copy("act", dstT.t[:], pv3, [pb.b], [dstT.b])
                    if src in (0, 1):
                        w = 0 if src == 1 else 1
                        k.copy("pool", arP.t[0:64, :, 0, w, :], dstT.t[0:64], [dstT.b], [arP.b])
                        k.copy("pool", arP.t[64:128, :, 1, w, :], dstT.t[64:128], [dstT.b], [arP.b])
                    elif src == 2:
                        k.copy("pool", bP.t[0:64, :, 0, :], dstT.t[0:64], [dstT.b], [bP.b])
                        k.copy("pool", bP.t[64:128, :, 1, :], dstT.t[64:128], [dstT.b], [bP.b])
                for hp in range(8):
                    grp, pi = divmod(hp, 2)
                    P0, PT0 = Pg[grp][0], PTg[grp][0]
                    pb = bank()
                    k.mm(pb.t[:], bT.t[:, hp, :], arP.t[:, hp].rearrange("p a w t -> p (a w t)"), True, True, [bT.b, arP.b], [pb.b])
                    bv = pb.t[:].rearrange("p (a w t) -> p a w t", a=2, w=2)
                    k.tt("dve", P0.t[:, 2 * pi:2 * pi + 2, :], bv[:, :, 0, :], cst.t[:, 0:1, :].broadcast_to([128, 2, 128]), ALU.mult,
                         [pb.b, cst.b], [P0.b])
                    k.tt("dve", Nbr.t[:, 2 * hp:2 * hp + 2, :], bv[:, :, 1, :], cst.t[:, 1:2, :].broadcast_to([128, 2, 128]), ALU.mult,
                         [pb.b, cst.b], [Nbr.b])
                    pb = bank()
                    k.mm(pb.t[:], kT.t[:, hp, :], arP.t[:, hp].rearrange("p a w t -> p (a w t)"), True, True, [kT.b, arP.b], [pb.b])
                    bv = pb.t[:].rearrange("p (a w t) -> p a w t", a=2, w=2)
                    k.tt("dve", MNk.t[:, hp], bv, cst.t[:, None, 0:2, :].broadcast_to([128, 2, 2, 128]), ALU.mult, [pb.b, cst.b], [MNk.b])
                    pb = bank()
                    k.mm(pb.t[:, 0:256], aT.t[:, hp, :], bP.t[:, hp].rearrange("p a t -> p (a t)"), True, True, [aT.b, bP.b], [pb.b])
                    k.tt("dve", PT0.t[:, 2 * pi:2 * pi + 2, :], pb.t[:, 0:256].rearrange("p (a t) -> p a t", a=2),
                         cst.t[:, 2:3, :].broadcast_to([128, 2, 128]), ALU.mult, [pb.b, cst.b], [PT0.b])
                def drain(nops):
                    for _ in range(nops):
                        if cq:
                            cq.pop(0)()

                for grp in range(4):
                    k.tt("dve", Y.t[:, 4 * grp:4 * grp + 4, :], Pg[grp][0].t[:], ident4, ALU.add, [Pg[grp][0].b, cst.b], [Y.b])
                drain(3)
                for lev in range(1, 7):
                    cur, nxt = (lev - 1) % 2, lev % 2
                    for grp in range(4):
                        Pc, PTc, Pn, PTn = Pg[grp][cur], PTg[grp][cur], Pg[grp][nxt], PTg[grp][nxt]
                        if lev < 6:
                            pb = bank()
                            for q in range(4):
                                k.mm(pb.t[:, q * 128:(q + 1) * 128], PTc.t[:, q, :], Pc.t[:, q, :], True, True, [PTc.b, Pc.b], [pb.b])
                            k.copy("act", Pn.t[:].rearrange("p a t -> p (a t)"), pb.t[:], [pb.b], [Pn.b])
                        pb = bank()
                        for q in range(4):
                            k.mm(pb.t[:, q * 128:(q + 1) * 128], Pc.t[:, q, :], PTc.t[:, q, :], True, True, [PTc.b, Pc.b], [pb.b])
                        k.copy("act", PTn.t[:].rearrange("p a t -> p (a t)"), pb.t[:], [pb.b], [PTn.b])
                    drain(1)
                    for grp in range(4):
                        PTn = PTg[grp][nxt]
                        pb = bank()
                        for q in range(4):
                            k.mm(pb.t[:, q * 128:(q + 1) * 128], PTn.t[:, q, :], Y.t[:, 4 * grp + q, :], True, True, [PTn.b, Y.b], [pb.b])
                        yv = Y.t[:, 4 * grp:4 * grp + 4, :]
                        k.tt("dve", yv, yv, pb.t[:].rearrange("p (a t) -> p a t", a=4), ALU.add, [pb.b, Y.b], [Y.b])
                    drain(2)
                drain(100)
                if idx + 1 < NCH:
                    load2a(idx + 1)
                if (RW_STOP or 99) < 4:
                    k.dma("sp", g.xTv[:, :, ci * 128:(ci + 1) * 128], X.t[:], X.b, reads=[X.b], writes=[g.bxT[ci // 2]])
                    continue
                for hf in range(2):
                    pb = bank()
                    for pp in range(4):
                        hp = hf * 4 + pp
                        k.mm(pb.t[:, pp * 128:(pp + 1) * 128], aT.t[:, hp, :], Sb.t[:, hp, :], True, False, [aT.b, Sb.b], [pb.b])
                        for h2 in range(2):
                            h = 2 * hp + h2
                            k.mm(pb.t[:, pp * 128 + h2 * 64:pp * 128 + (h2 + 1) * 64], MNk.t[:, hp, h2, 0, :], SG.t[:, 6, h * 64:(h + 1) * 64],
                                 False, h2 == 1, [MNk.b, SG.b], [pb.b])
                    k.copy("act", WT.t[:, hf * 512:(hf + 1) * 512], pb.t[:], [pb.b], [WT.b])
                for hf in range(2):
                    pb = bank()
                    for q in range(8):
                        h = hf * 8 + q
                        k.mm(pb.t[:, q * 64:(q + 1) * 64], Y.t[:, h, :], WT.t[:, h * 64:(h + 1) * 64], True, True, [Y.b, WT.b], [pb.b])
                    k.copy("act", UTb.t[:, hf * 512:(hf + 1) * 512], pb.t[:], [pb.b], [UTb.b])
                for hf in range(2):
                    pb = bank()
                    for pp in range(4):
                        hp = hf * 4 + pp
                        k.mm(pb.t[:, pp * 128:(pp + 1) * 128], rT.t[:, hp, :], Sb.t[:, hp, :], True, False, [rT.b, Sb.b], [pb.b])
                        for h2 in range(2):
                            h = 2 * hp + h2
                            cs = slice(pp * 128 + h2 * 64, pp * 128 + (h2 + 1) * 64)
                            k.mm(pb.t[:, cs], Nbr.t[:, h, :], UTb.t[:, h * 64:(h + 1) * 64], False, False, [Nbr.b, UTb.b], [pb.b])
                            k.mm(pb.t[:, cs], MNk.t[:, hp, h2, 1, :], SG.t[:, 6, h * 64:(h + 1) * 64], False, h2 == 1, [MNk.b, SG.b], [pb.b])
                    k.copy("act", o32.t[:, hf * 512:(hf + 1) * 512], pb.t[:], [pb.b], [o32.b])
                for hf in range(2):
                    pb = bank()
                    for pp in range(4):
                        hp = hf * 4 + pp
                        cs = slice(pp * 128, (pp + 1) * 128)
                        k.mm(pb.t[:, cs], SG.t[:, 4, hp * 128:(hp + 1) * 128], UTb.t[:, hp * 128:(hp + 1) * 128], True, False, [SG.b, UTb.b], [pb.b])
                        k.mm(pb.t[:, cs], SG.t[:, 5, hp * 128:(hp + 1) * 128], SG.t[:, 6, hp * 128:(hp + 1) * 128], False, True, [SG.b], [pb.b])
                    for h2 in range(2):
                        rs = slice(h2 * 64, (h2 + 1) * 64)
                        sv = S_.t[rs, hf * 4:(hf + 1) * 4, h2 * 64:(h2 + 1) * 64]
                        k.tt("dve", sv, sv, gC_all.t[rs, ci, hf * 4:(hf + 1) * 4].unsqueeze(2).broadcast_to([64, 4, 64]), ALU.mult,
                             [S_.b, gC_all.b, Sb.b], [S_.b])
                        k.tt("dve", sv, sv, pb.t[rs, :].rearrange("p (c x) -> p c x", c=4)[:, :, h2 * 64:(h2 + 1) * 64], ALU.add, [pb.b, S_.b], [S_.b])
                        k.copy("dve", Sb.t[rs, hf * 4:(hf + 1) * 4, h2 * 64:(h2 + 1) * 64], sv, [S_.b], [Sb.b])
                def mk_tail(ci=ci, sq=sq, SG=SG, G=G, X=X):
                    ops = []
                    o3 = v3(o32.t[:])
                    ops.append(lambda: k.op("dve", lambda e: e.tensor_reduce(small.t[:, 0, :], o3, AX.X, ALU.add), [o32.b], [small.b]))
                    ops.append(lambda: k.tt("pool", E1.t[:], o32.t[:], o32.t[:], ALU.mult, [o32.b], [E1.b]))
                    ops.append(lambda: k.op("dve", lambda e: e.tensor_reduce(small.t[:, 1, :], v3(E1.t[:]), AX.X, ALU.add), [E1.b], [small.b]))

                    def stats():
                        k.ts("dve", small.t[:, 0, :], small.t[:, 0, :], 1.0 / 64, None, ALU.mult, None, [small.b], [small.b])
                        k.tt("dve", small.t[:, 2, :], small.t[:, 0, :], small.t[:, 0, :], ALU.mult, [small.b], [small.b])
                        k.stt(small.t[:, 1, :], small.t[:, 1, :], 1.0 / 64, small.t[:, 2, :], ALU.mult, ALU.subtract, [small.b], [small.b])
                        k.act(small.t[:, 1, :], small.t[:, 1, :], AF.Sqrt, [small.b, g.bconst], [small.b], bias=g.epsc[:, 2:3])
                    ops.append(stats)
                    ops.append(lambda: k.tt("pool", v3(E2.t[:]), v3(SG.t[:, 6, :]), bc_all.t[:, ci, :].unsqueeze(2).broadcast_to([128, 16, 64]), ALU.mult,
                                            [SG.b, bc_all.b], [E2.b]))
                    ops.append(lambda: k.recip(small.t[:, 1, :], small.t[:, 1, :], [small.b], [small.b]))
                    ops.append(lambda: k.tt("dve", v3(E1.t[:]), o3, small.t[:, 0, :].unsqueeze(2).broadcast_to([128, 16, 64]), ALU.subtract, [o32.b, small.b], [E1.b]))
                    ops.append(lambda: k.tt("dve", v3(E1.t[:]), v3(E1.t[:]), small.t[:, 1, :].unsqueeze(2).broadcast_to([128, 16, 64]), ALU.mult, [E1.b, small.b], [E1.b]))
                    ops.append(lambda: k.tt("pool", E1.t[:], E1.t[:], lnb.t[:, 0, :], ALU.mult, [E1.b, lnb.b], [E1.b]))
                    ops.append(lambda: k.tt("pool", E1.t[:], E1.t[:], lnb.t[:, 1, :], ALU.add, [E1.b, lnb.b], [E1.b]))
                    ops.append(lambda: k.tt("dve", E1.t[:], E1.t[:], E2.t[:], ALU.add, [E1.b, E2.b], [E1.b]))
                    ops.append(lambda: k.tt("dve", og.t[:], E1.t[:], G.t[:], ALU.mult, [E1.b, G.b], [og.b]))

                    def pe_part():
                        pb = pstb
                        pv = pb.t[:]
                        for c in range(8):
                            k.tr(pv[:, c * 128:(c + 1) * 128], og.t[:, c * 128:(c + 1) * 128], g.identb[:], [og.b, g.bconst], [pb.b])
                        k.copy("act", ogT.t[:].rearrange("p c t -> p (c t)"), pv, [pb.b], [ogT.b])
                        for hf in range(2):
                            pb = bank()
                            for q in range(4):
                                dc = hf * 4 + q
                                for c in range(8):
                                    k.mm(pb.t[:, q * 128:(q + 1) * 128], wo.t[:, c, dc * 128:(dc + 1) * 128], ogT.t[:, c, :], c == 0, c == 7,
                                         [wo.b, ogT.b], [pb.b])
                            for q in range(4):
                                dc = hf * 4 + q
                                k.stt(X.t[:, dc, :], pb.t[:, q * 128:(q + 1) * 128], g.hg[:, i, sq, 1, dc:dc + 1], X.t[:, dc, :], ALU.mult, ALU.add,
                                      [pb.b, g.bmod, X.b], [X.b])
                        k.dma("sp", g.xTv[:, :, ci * 128:(ci + 1) * 128], X.t[:], X.b, reads=[X.b], writes=[g.bxT[ci // 2]])
                    return ops, pe_part

                if pe_prev[0] is not None:
                    pe_prev[0]()
                if idx + 1 < NCH:
                    load2b(idx + 1)
                ops, pe_part = mk_tail()
                cq.extend(ops)
                pe_prev[0] = pe_part
                if RW_STOP == 7:
                    while cq:
                        cq.pop(0)()
                    pe_part()
                    pe_prev[0] = None
            while cq:
                cq.pop(0)()
            if pe_prev[0] is not None:
                pe_prev[0]()
            k.barrier()
    k.pes = outer


def phase_swa(g, i):
    k = g.k
    j = i // 3
    N = 256
    nb = NT // N
    wq = k.sb([128, 8, 1024], BF16)
    wkd = k.sb([128, 8, 2, 128], BF16)
    wv = k.sb([128, 8, 128], BF16)
    wo = k.sb([128, 8, D], BF16)
    bw = Buf()
    wsrc = g.sw_w_qkv[j].rearrange("(c p) n -> p c n", p=128)
    k.dma("pool", wq[:], wsrc[:, :, 0:1024], bw, writes=[bw])
    for kv in range(2):
        for r in range(2):
            k.dma("pool", wkd[:, :, kv, r * 64:(r + 1) * 64], wsrc[:, :, 1024 + kv * 64:1024 + (kv + 1) * 64], bw, writes=[bw])
    k.dma("pool", wv[:], wsrc[:, :, 1152:1280], bw, writes=[bw])
    k.dma("pool", wo[:], g.sw_w_o[j].rearrange("(c p) n -> p c n", p=128), bw, writes=[bw])
    gq = k.sb([128, 2], F32)
    bgq = Buf()
    load_vec128(g, gq[:, 0:1], g.sw_q_norm_g[j], bgq)
    load_vec128(g, gq[:, 1:2], g.sw_k_norm_g[j], bgq)
    k.ts("dve", gq[:, 0:1], gq[:, 0:1], 0.125, None, ALU.mult, None, [bgq], [bgq])
    es = k.sb([128, 16], F32)
    k.dma("sp", es[:], g.sw_sinks[j:j + 1].broadcast_to([128, 16]), bgq, writes=[bgq])
    k.act(es[:], es[:], AF.Exp, [bgq], [bgq])
    io = k.sb([128, 128], I32)
    k.op("pool", lambda e: e.iota(io[:], [[1, 128]], base=0, channel_multiplier=-1), (), [bgq])
    masks = k.sb([128, 2, 128], BF16)
    k.ts("dve", masks[:, 0, :], io[:], 0, None, ALU.is_lt, None, [bgq], [bgq])
    k.ts("dve", masks[:, 1, :], io[:], 0, None, ALU.is_ge, None, [bgq], [bgq])
    xt = [k.sb([128, 8, N], F32) for _ in range(2)]
    bxt = [Buf(), Buf()]
    hT = k.sb([128, 8, N], BF16)
    bhT = Buf()
    qT = k.sb([128, 8, N], BF16)
    bqT = Buf()
    kTd = [k.sb([128, 2, N], BF16) for _ in range(2)]
    bkT = [Buf(), Buf()]
    va = [k.sb([128, 2, 2, 65], BF16) for _ in range(2)]
    bva = [Buf(), Buf()]
    for s in range(2):
        k.memset("dve", va[s][:, :, :, 64:65], 1.0, [bva[s]])
    st = norm_scratch(g, N)
    qs = qk_scratch(g, N, nps=0)
    banks = [TB(k.psum([128, 512], F32)) for _ in range(6)]
    bi = [0]

    def bank():
        bb = banks[bi[0] % 6]
        bi[0] += 1
        return bb

    pst = k.psum([128, 128], BF16)
    bpst = Buf()
    pT = [k.sb([128, 4, 128], BF16) for _ in range(4)]
    bpT = [Buf() for _ in range(4)]
    l4 = [k.sb([128, 4], F32) for _ in range(2)]
    bl4 = [Buf(), Buf()]
    otok = k.sb([128, D], BF16)
    botok = Buf()
    oT = k.sb([128, 8, N], BF16)
    boT = Buf()

    def load(b):
        s = b % 2
        k.dma("sp", xt[s][:], g.xTv[:, :, b * N:(b + 1) * N], bxt[s], reads=[g.bxT[b]], writes=[bxt[s]])

    load(0)
    ip = 0
    il = 0
    for b in range(nb):
        s = b % 2
        sq, t0 = divmod(b * N, T)
        if b + 1 < nb:
            load(b + 1)
        norm_mod(g, xt[s], bxt[s], hT, bhT, i, sq, 1, N, st)
        jobs = [(wq[:, :, cq * 128:(cq + 1) * 128], qT[:, cq, :], gq[:, 0:1], bqT) for cq in range(8)]
        jobs += [(wkd[:, :, kv, :], kTd[s][:, kv, :], gq[:, 1:2], bkT[s]) for kv in range(2)]
        nj = len(jobs)
        pj = [None] * nj
        sj = [None] * nj
        for ii in range(nj + 2):
            if ii < nj:
                pj[ii] = bank()
                for c in range(8):
                    k.mm(pj[ii].t[:, :N], jobs[ii][0][:, c, :], hT[:, c, :], c == 0, c == 7, [bw, bhT], [pj[ii].b])
            if 1 <= ii <= nj:
                q_ = ii - 1
                s2 = q_ % 2
                k.act(qs["sq"][s2][:, :N], pj[q_].t[:, :N], AF.Square, [pj[q_].b], [qs["bsq"][s2]])
                sj[q_] = bank()
                k.mm(sj[q_].t[:, :N], g.blk64[:], qs["sq"][s2][:, :N], True, True, [qs["bsq"][s2], g.bconst], [sj[q_].b])
            if ii >= 2:
                q_ = ii - 2
                s2 = q_ % 2
                _, outap, gvec, bdst = jobs[q_]
                k.act(qs["rt"][s2][:, :N], sj[q_].t[:, :N], AF.Sqrt, [sj[q_].b, g.bconst], [qs["brt"][s2]], bias=g.epsc[:, 0:1], scale=1.0 / 64)
                k.recip(qs["rs"][s2][:, :N], qs["rt"][s2][:, :N], [qs["brt"][s2]], [qs["brs"][s2]])
                k.stt(outap, pj[q_].t[:, :N], gvec, qs["rs"][s2][:, :N], ALU.mult, ALU.mult, [pj[q_].b, qs["brs"][s2], bgq], [bdst])
        for jt in range(2):
            pv_ = bank()
            for c in range(8):
                k.mm(pv_.t[:, 0:128], hT[:, c, jt * 128:(jt + 1) * 128], wv[:, c, :], c == 0, c == 7, [bw, bhT], [pv_.b])
            k.copy("dve", va[s][:, jt, :, 0:64], pv_.t[:, 0:128].rearrange("p (a d) -> p a d", a=2), [pv_.b], [bva[s]])
        for jt in range(2):
            tiles = []
            if jt == 1:
                tiles.append((s, 0, 0))
            elif t0 > 0:
                tiles.append((1 - s, 1, 0))
            tiles.append((s, jt, 1))
            for kv in range(2):
                for half in range(2):
                    pts = []
                    for (ks, ktile, mi) in tiles:
                        psb = bank()
                        pp = ip % 4
                        ip += 1
                        for hh in range(4):
                            hq = kv * 8 + half + 2 * hh
                            r0 = (hq % 2) * 64
                            k.mm(psb.t[:, hh * 128:(hh + 1) * 128], kTd[ks][r0:r0 + 64, kv, ktile * 128:(ktile + 1) * 128],
                                 qT[r0:r0 + 64, hq // 2, jt * 128:(jt + 1) * 128], True, True, [bkT[ks], bqT], [psb.b])
                        k.act(pT[pp][:].rearrange("p a q -> p (a q)"), psb.t[:], AF.Exp, [psb.b], [bpT[pp]])
                        k.tt("pool", pT[pp][:], pT[pp][:], masks[:, mi:mi + 1, :].broadcast_to([128, 4, 128]), ALU.mult,
                             [bpT[pp], bgq], [bpT[pp]])
                        pts.append((pp, ks, ktile))
                    accb = bank()
                    acc, bacc = accb.t, accb.b
                    for hh in range(4):
                        for ti, (pp, ks, ktile) in enumerate(pts):
                            k.mm(acc[:, hh * 65:(hh + 1) * 65], pT[pp][:, hh, :], va[ks][:, ktile, kv, :], ti == 0, ti == len(pts) - 1,
                                 [bpT[pp], bva[ks]], [bacc])
                    li = il % 2
                    il += 1
                    h0 = kv * 8 + half
                    accv = acc[:, 0:260].rearrange("p (a e) -> p a e", e=65)
                    k.tt("dve", l4[li][:], accv[:, :, 64], es[:, h0:h0 + 7:2], ALU.add, [bacc, bgq], [bl4[li]])
                    k.recip(l4[li][:], l4[li][:], [bl4[li]], [bl4[li]])
                    k.tt("dve", otok[:].rearrange("p (a d) -> p a d", d=64)[:, h0:h0 + 7:2, :], accv[:, :, 0:64],
                         l4[li][:].unsqueeze(2).broadcast_to([128, 4, 64]), ALU.mult, [bacc, bl4[li]], [botok])
            for c in range(8):
                k.tr(pst[:], otok[:, c * 128:(c + 1) * 128], g.identb[:], [botok, g.bconst], [bpst])
                k.copy("act", oT[:, c, jt * 128:(jt + 1) * 128], pst[:], [bpst], [boT])
        for dc in range(8):
            pob = bank()
            for c in range(8):
                k.mm(pob.t[:, :N], wo[:, c, dc * 128:(dc + 1) * 128], oT[:, c, :], c == 0, c == 7, [bw, boT], [pob.b])
            k.stt(xt[s][:, dc, :], pob.t[:, :N], g.hg[:, i, sq, 1, dc:dc + 1], xt[s][:, dc, :], ALU.mult, ALU.add,
                  [pob.b, g.bmod, bxt[s]], [bxt[s]])
        k.dma("sp", g.xTv[:, :, b * N:(b + 1) * N], xt[s][:], bxt[s], reads=[bxt[s]], writes=[g.bxT[b]])


_IN_NAMES = ["norm_g", "ada_w", "ada_b", "ffn_w_in", "ffn_w_out", "da_w_qkv", "da_w_o", "da_q_norm_g", "da_k_norm_g",
             "da_lambda", "da_subln_g", "rw_mu", "rw_w_rkv", "rw_w_o", "rw_decay_w0", "rw_decay_w1", "rw_decay_w2",
             "rw_iclr_a0", "rw_iclr_a1", "rw_iclr_a2", "rw_gate_g1", "rw_gate_g2", "rw_k_k", "rw_k_a", "rw_r_k",
             "rw_lnx_g", "rw_lnx_b", "sw_w_qkv", "sw_w_o", "sw_q_norm_g", "sw_k_norm_g", "sw_sinks"]


def make_in_maps(inputs, ncores=NCORES):
    shared = {n: np.ascontiguousarray(inputs[n], dtype=np.float32) for n in _IN_NAMES}
    maps = []
    for cidx in range(ncores):
        m = dict(shared)
        m["x"] = np.ascontiguousarray(inputs["x"][cidx * NS:(cidx + 1) * NS], dtype=np.float32)
        m["c"] = np.ascontiguousarray(inputs["c"][cidx * NS:(cidx + 1) * NS], dtype=np.float32)
        maps.append(m)
    return maps


def kernel(**inputs):
    nc = build()
    maps = make_in_maps(inputs)
    res = run_bass_kernel_spmd(nc, maps, core_ids=list(range(NCORES)))
    return np.concatenate([r["y"] for r in res.results], axis=0).astype(np.float32)
```
